# Optimizing a Trainium2 kernel written in Bass

```python
import math
import jax
import jax.numpy as jnp
from jax import lax
import numpy as np

D_MODEL = 1024
BATCH = 8
SEQ = 4096
DEPTH = 4

A_WIDTH = 256
A_GROUPS = 4
A_GDIM = A_WIDTH // A_GROUPS
CHUNK = 128
B_WIDTH = 256
POOL_WINDOWS = (2, 4, 8, 16)
B_GROUPS = len(POOL_WINDOWS)
B_GDIM = B_WIDTH // B_GROUPS
C_HEADS = 4
C_HEAD_DIM = 64
C_VDIM = 2 * C_HEAD_DIM
C_QK_WIDTH = C_HEADS * 2 * C_HEAD_DIM
C_V_WIDTH = C_HEADS * C_VDIM
Q_BLOCK = 128
N_BUCKETS = 32
MAX_EXACT = 16
MAX_DISTANCE = 128
IN_COLS = 2 * A_WIDTH + B_WIDTH + 2 * C_QK_WIDTH + C_V_WIDTH
N_BRANCH = 3
D_FF = 2816
CONV_WIDTH = 3
EPS = 1e-6

kernel_name = 'hybrid_gated_parallel_block'


def rmsnorm(x, g, eps=EPS):
    xf = x.astype(jnp.float32)
    y = xf * lax.rsqrt(jnp.mean(xf * xf, axis=-1, keepdims=True) + eps)
    return (y * g.astype(jnp.float32)).astype(x.dtype)


def layernorm(x, g, b, eps=1e-5):
    xf = x.astype(jnp.float32)
    mu = jnp.mean(xf, axis=-1, keepdims=True)
    var = jnp.mean(jnp.square(xf - mu), axis=-1, keepdims=True)
    y = (xf - mu) * lax.rsqrt(var + eps)
    return (y * g.astype(jnp.float32) + b.astype(jnp.float32)).astype(x.dtype)


def gmlp_mixer(zuv, ln_g, ln_b, w_s, b_s):
    B, S, _ = zuv.shape
    z = jax.nn.gelu(zuv, approximate=False)
    u, v = z[..., :A_WIDTH], z[..., A_WIDTH:]
    v = layernorm(v, ln_g, ln_b)
    vc = v.reshape(B, S // CHUNK, CHUNK, A_GROUPS, A_GDIM)
    mask = jnp.tril(jnp.ones((CHUNK, CHUNK), dtype=bool))
    ws = jnp.where(mask[None], w_s, jnp.zeros_like(w_s))
    mixed = jnp.einsum('gts,bcsgd->bctgd', ws, vc) + b_s.T[None, None, :, :, None]
    return u * mixed.reshape(B, S, A_WIDTH)


def pool_mixer(xb, pool_w, pool_scale):
    B, S, _ = xb.shape
    xf = xb.astype(jnp.float32)
    cs = jnp.cumsum(xf, axis=1)
    t = jnp.arange(S)
    outs = []
    for gi, w in enumerate(POOL_WINDOWS):
        sl = slice(gi * B_GDIM, (gi + 1) * B_GDIM)
        c = cs[..., sl]
        lag = jnp.pad(c, ((0, 0), (w, 0), (0, 0)))[:, :S]
        cnt = jnp.minimum(t + 1, w).astype(jnp.float32)[None, :, None]
        outs.append((c - lag) / cnt - xf[..., sl])
    p = jnp.stack(outs, axis=2).astype(xb.dtype)
    y = jnp.einsum('bsgc,gcd->bsgd', p, pool_w).reshape(B, S, B_WIDTH)
    return y * pool_scale


def t5_causal_bucket(rel):
    n = jnp.maximum(rel, 0)
    nf = jnp.maximum(n, 1).astype(jnp.float32)
    large = MAX_EXACT + (jnp.log(nf / MAX_EXACT) / math.log(MAX_DISTANCE / MAX_EXACT)
                         * (N_BUCKETS - MAX_EXACT)).astype(jnp.int32)
    large = jnp.minimum(large, N_BUCKETS - 1)
    return jnp.where(n < MAX_EXACT, n, large)


def diff_attention(q, k, v, lam, rel_bias):
    B, S, H, _, dh = q.shape
    nb = S // Q_BLOCK
    scale = dh ** -0.5
    qb = q.reshape(B, nb, Q_BLOCK, H, 2, dh).transpose(1, 0, 2, 3, 4, 5)
    kpos = jnp.arange(S)

    def block(args):
        qi, i = args
        qpos = i * Q_BLOCK + jnp.arange(Q_BLOCK)
        rel = qpos[:, None] - kpos[None, :]
        bias = rel_bias[t5_causal_bucket(rel)].astype(jnp.float32)
        s = jnp.einsum('bqhcd,bkhcd->bhcqk', qi, k).astype(jnp.float32) * scale
        s = s + bias.transpose(2, 0, 1)[None, :, None]
        s = jnp.where((rel >= 0)[None, None, None], s, -1e30)
        p = jax.nn.softmax(s, axis=-1)
        a = p[:, :, 0] - lam * p[:, :, 1]
        return jnp.einsum('bhqk,bkhd->bqhd', a.astype(v.dtype), v)

    out = lax.map(block, (qb, jnp.arange(nb)))
    return out.transpose(1, 0, 2, 3, 4).reshape(B, S, H, v.shape[-1])


def diff_attn_mixer(zq, zk, zv, lam_vecs, subln_g, rel_bias, layer_idx):
    B, S, _ = zq.shape
    q = zq.reshape(B, S, C_HEADS, 2, C_HEAD_DIM)
    k = zk.reshape(B, S, C_HEADS, 2, C_HEAD_DIM)
    v = zv.reshape(B, S, C_HEADS, C_VDIM)
    lam_init = 0.8 - 0.6 * math.exp(-0.3 * layer_idx)
    lv = lam_vecs.astype(jnp.float32)
    lam = jnp.exp(jnp.sum(lv[0] * lv[1])) - jnp.exp(jnp.sum(lv[2] * lv[3])) + lam_init
    o = diff_attention(q, k, v, lam, rel_bias)
    o = rmsnorm(o, subln_g, eps=1e-5) * (1.0 - lam_init)
    return o.reshape(B, S, C_V_WIDTH)


def conv_ffn(h, w_up, conv_w, conv_b, w_down):
    gu = h @ w_up
    gate, up = gu[..., :D_FF], gu[..., D_FF:]
    gate = lax.conv_general_dilated(
        gate, conv_w[:, None, :], window_strides=(1,), padding=[(CONV_WIDTH - 1, 0)],
        dimension_numbers=('NWC', 'WIO', 'NWC'), feature_group_count=D_FF) + conv_b
    return (jax.nn.gelu(gate, approximate=False) * up) @ w_down


def setup_inputs(seed: int = 0) -> dict:
    key = jax.random.key(seed)
    ks = jax.random.split(key, 24)
    L, D = DEPTH, D_MODEL

    def nrm(k, shape, scale):
        return jax.random.normal(k, shape, jnp.float32) * scale

    def gain(k, shape):
        return 1.0 + nrm(k, shape, 0.05)

    return {
        'x': nrm(ks[0], (BATCH, SEQ, D), 1.0),
        'attn_norm_g': gain(ks[1], (L, D)),
        'w_in': nrm(ks[2], (L, D, IN_COLS), D ** -0.5),
        'w_gate': nrm(ks[3], (L, D, N_BRANCH * D), D ** -0.5),
        'b_gate': nrm(ks[4], (L, N_BRANCH * D), 0.01),
        'sgu_ln_g': gain(ks[5], (L, A_WIDTH)),
        'sgu_ln_b': nrm(ks[6], (L, A_WIDTH), 0.01),
        'sgu_w': nrm(ks[7], (L, A_GROUPS, CHUNK, CHUNK), CHUNK ** -0.5),
        'sgu_b': gain(ks[8], (L, A_GROUPS, CHUNK)),
        'proj_a': nrm(ks[9], (L, A_WIDTH, D), A_WIDTH ** -0.5),
        'pool_w': nrm(ks[10], (L, B_GROUPS, B_GDIM, B_GDIM), B_GDIM ** -0.5),
        'pool_scale': gain(ks[11], (L, B_WIDTH)),
        'proj_b': nrm(ks[12], (L, B_WIDTH, D), B_WIDTH ** -0.5),
        'diff_lam': nrm(ks[13], (L, 4, C_HEAD_DIM), 0.1),
        'diff_subln_g': gain(ks[14], (L, C_VDIM)),
        'proj_c': nrm(ks[15], (L, C_V_WIDTH, D), C_V_WIDTH ** -0.5),
        'w_out': nrm(ks[16], (L, D, D), D ** -0.5),
        'ffn_norm_g': gain(ks[17], (L, D)),
        'w_up': nrm(ks[18], (L, D, 2 * D_FF), D ** -0.5),
        'conv_w': nrm(ks[19], (L, CONV_WIDTH, D_FF), CONV_WIDTH ** -0.5),
        'conv_b': nrm(ks[20], (L, D_FF), 0.01),
        'w_down': nrm(ks[21], (L, D_FF, D), D_FF ** -0.5),
        'rel_bias': nrm(ks[22], (N_BUCKETS, C_HEADS), 0.5),
        'final_norm_g': gain(ks[23], (D,)),
    }


def reference(x, attn_norm_g, w_in, w_gate, b_gate, sgu_ln_g, sgu_ln_b, sgu_w, sgu_b,
              proj_a, pool_w, pool_scale, proj_b, diff_lam, diff_subln_g, proj_c, w_out,
              ffn_norm_g, w_up, conv_w, conv_b, w_down, rel_bias, final_norm_g):
    o_a = 2 * A_WIDTH
    o_b = o_a + B_WIDTH
    o_q = o_b + C_QK_WIDTH
    o_k = o_q + C_QK_WIDTH
    for l in range(DEPTH):
        h = rmsnorm(x, attn_norm_g[l])
        z = h @ w_in[l]
        gates = jax.nn.sigmoid(h @ w_gate[l] + b_gate[l])
        y_a = gmlp_mixer(z[..., :o_a], sgu_ln_g[l], sgu_ln_b[l], sgu_w[l], sgu_b[l])
        y_b = pool_mixer(z[..., o_a:o_b], pool_w[l], pool_scale[l])
        y_c = diff_attn_mixer(z[..., o_b:o_q], z[..., o_q:o_k], z[..., o_k:],
                              diff_lam[l], diff_subln_g[l], rel_bias, l)
        g_a = gates[..., :D_MODEL]
        g_b = gates[..., D_MODEL:2 * D_MODEL]
        g_c = gates[..., 2 * D_MODEL:]
        merged = g_a * (y_a @ proj_a[l]) + g_b * (y_b @ proj_b[l]) + g_c * (y_c @ proj_c[l])
        x = x + merged @ w_out[l]
        x = x + conv_ffn(rmsnorm(x, ffn_norm_g[l]), w_up[l], conv_w[l], conv_b[l], w_down[l])
    return rmsnorm(x, final_norm_g)
```

```python
import math
from contextlib import ExitStack

import numpy as np
import concourse.bass as bass
import concourse.mybir as mybir
from concourse.bass_utils import run_bass_kernel_spmd

F32 = mybir.dt.float32
BF16 = mybir.dt.bfloat16
AF = mybir.ActivationFunctionType
ALU = mybir.AluOpType

D = 1024
DFF = 2816
NFF = DFF // 128
T = 512
NCOLV = 131


class Res:
    __slots__ = ("name", "lw", "rd", "chan")

    def __init__(self, name):
        self.name = name
        self.lw = None
        self.rd = []
        self.chan = None


class Ins:
    __slots__ = ("eng", "fn", "deps", "sig", "val", "is_dma", "chan", "epoch")


class Prog:
    ENGS = ("pe", "act", "dve", "pool", "sp")

    def __init__(self, nc, n_epochs=1, same_engine_sync=True):
        self.nc = nc
        self.q = {e: [] for e in self.ENGS}
        self.n_epochs = n_epochs
        self.epoch = 0
        self.same_engine_sync = same_engine_sync
        self.chans = []

    def res(self, name):
        return Res(name)

    def op(self, eng, fn, reads=(), writes=(), dma=False):
        ins = Ins()
        ins.eng = eng
        ins.fn = fn
        ins.is_dma = dma
        ins.epoch = self.epoch
        ins.sig = False
        ins.val = None
        ins.chan = None
        deps = set()
        for r in reads:
            if r.lw is not None:
                deps.add(r.lw)
        for w in writes:
            if w.lw is not None:
                deps.add(w.lw)
            last = {}
            for rdr in w.rd:
                if rdr.is_dma:
                    deps.add(rdr)
                else:
                    last[rdr.eng] = rdr
            deps.update(last.values())
        deps.discard(ins)
        for r in reads:
            r.rd.append(ins)
        for w in writes:
            w.lw = ins
            w.rd = []
        if dma:
            assert len(writes) == 1
            w = writes[0]
            if w.chan is None:
                w.chan = {"sem": None, "count": 0, "name": w.name}
                self.chans.append(w.chan)
            ins.chan = w.chan
        pruned = set()
        for d in deps:
            if d.is_dma:
                pruned.add(d)
            elif d.eng == eng:
                if eng in ("pe", "sp"):
                    continue
                if dma or self.same_engine_sync:
                    pruned.add(d)
            else:
                pruned.add(d)
        ins.deps = pruned
        for d in pruned:
            d.sig = True
        self.q[eng].append(ins)
        return ins

    def emit(self, stack):
        nc = self.nc
        esem = {}
        for e in self.ENGS:
            for ep in range(self.n_epochs):
                esem[(e, ep)] = stack.enter_context(nc.semaphore(f"s_{e}_{ep}"))
        for i, ch in enumerate(self.chans):
            ch["sem"] = stack.enter_context(nc.semaphore(f"d_{i}"))
        cnt = {}
        for e in self.ENGS:
            for ins in self.q[e]:
                if ins.is_dma:
                    ins.chan["count"] += 16
                    ins.val = ins.chan["count"]
                elif ins.sig:
                    k = (e, ins.epoch)
                    cnt[k] = cnt.get(k, 0) + 1
                    ins.val = cnt[k]
        self.stats = {e: len(self.q[e]) for e in self.ENGS}
        self.stats["cnt"] = {f"{k[0]}{k[1]}": v for k, v in cnt.items()}
        block = stack.enter_context(nc.Block())

        def run(e, engobj):
            waited = {}
            nw = 0
            for ins in self.q[e]:
                need = {}
                for d in ins.deps:
                    if d.is_dma:
                        key = ("dma", id(d.chan))
                        sem = d.chan["sem"]
                    else:
                        key = (d.eng, d.epoch)
                        sem = esem[key]
                    v = d.val
                    if waited.get(key, 0) >= v:
                        continue
                    if key not in need or need[key][1] < v:
                        need[key] = (sem, v)
                for key, (sem, v) in need.items():
                    engobj.wait_ge(sem, v)
                    waited[key] = v
                    nw += 1
                bi = ins.fn(engobj)
                if bi is None:
                    continue
                if ins.is_dma:
                    bi.then_inc(ins.chan["sem"], 16)
                elif ins.sig:
                    bi.then_inc(esem[(e, ins.epoch)], 1)
            self.stats["waits_" + e] = nw

        block.sync(lambda eng: run("sp", eng))
        block.tensor(lambda eng: run("pe", eng))
        block.scalar(lambda eng: run("act", eng))
        block.vector(lambda eng: run("dve", eng))
        block.gpsimd(lambda eng: run("pool", eng))


def build(S=4096, depth=4, dbg=False, layer0=0):
    NT = S // T
    nc = bass.Bass("TRN2", target_bir_lowering=False)
    P = Prog(nc, n_epochs=depth + 1)
    st = ExitStack()

    def din(name, shape, dt=F32):
        return nc.dram_tensor(name, list(shape), dt, kind="ExternalInput").ap()

    xT_d = din("xT", [D, S])
    w_in_d = din("w_in", [depth, D, 2304])
    w_gate_d = din("w_gate", [depth, D, 3072])
    proj_a_d = din("proj_a", [depth, 256, D])
    proj_b_d = din("proj_b", [depth, 256, D])
    proj_c_d = din("proj_c", [depth, 512, D])
    w_out_d = din("w_out", [depth, D, D])
    w_up_d = din("w_up", [depth, D, 2 * DFF])
    w_down_d = din("w_down", [depth, DFF, D])
    NCOL = depth * NCOLV + 8
    colv_d = din("colv", [128, NCOL])
    lamrep_d = din("lamrep", [128, depth * 256])
    b31_d = din("b31rep", [128, 4])
    dbias_d = din("dbias", [128, 4, 2, 128])
    maskneg_d = din("maskneg", [128, 128])
    tril_d = din("tril01", [128, 128])
    lnrep_d = din("lnrep", [depth, 128, 2, 256])
    bsT_d = din("bsT", [depth, 128, 2, 128])
    wsT_d = din("wsT", [depth, 128, 4, 128])
    poolbd_d = din("poolbd", [depth, 128, 2, 128])
    constk_d = din("constk", [128, 34])
    outT_d = nc.dram_tensor("outT", [D, S], F32, kind="ExternalOutput").ap()

    win_bf = nc.dram_tensor("win_bf", [depth, D, 2304], BF16).ap()
    wgate_bf = nc.dram_tensor("wgate_bf", [depth, D, 3072], BF16).ap()
    proj_bf = nc.dram_tensor("proj_bf", [depth, D, D], BF16).ap()
    wout_bf = nc.dram_tensor("wout_bf", [depth, D, D], BF16).ap()
    wup_bf = nc.dram_tensor("wup_bf", [depth, D, 2 * DFF], BF16).ap()
    wdn_bf = nc.dram_tensor("wdn_bf", [depth, DFF, D], BF16).ap()
    xs_d = nc.dram_tensor("xs", [NT, 128, 8, T], F32).ap()
    r_xs = P.res("xs")
    r_out = P.res("out")

    dbg_out = {}

    def sb(name, shape, dt):
        return nc.alloc_sbuf_tensor("s_" + name, list(shape), dt)

    xT = sb("xTt", [128, 8, T], F32)
    r_xTall = P.res("xT")
    r_xT = [r_xTall] * 8
    hT = sb("hT", [128, 8, T], BF16)
    r_hT = [P.res(f"hT{k}") for k in range(8)]
    KT = sb("KT", [128, 4, S], BF16)
    r_KT = [[P.res(f"KT{h}_{i}") for i in range(NT)] for h in range(4)]
    V = sb("V", [128, S // 128, 512], BF16)
    r_V = [P.res(f"V{kb}") for kb in range(S // 128)]
    QT = sb("QT", [128, 4, T], BF16)
    r_QT = [P.res(f"QT{h}") for h in range(4)]
    vA = sb("vA", [128, 4, 256], BF16)
    vB = sb("vB", [128, 4, 256], BF16)
    r_vAB = [P.res(f"vAB{s}") for s in range(4)]
    XB = sb("XB", [128, 2, 16 + T], F32)
    r_XB = [P.res(f"XB{c}") for c in range(2)]
    pb = sb("pb", [128, NFF, T], BF16)
    r_pb = [P.res(f"pb{j}") for j in range(NFF)]
    NSLOT = 5
    wslot = [sb(f"wslot{i}", [128, 8, 512], BF16) for i in range(NSLOT)]
    r_wslot = [P.res(f"wslot{i}") for i in range(NSLOT)]
    NSF = 10
    sf_t = [sb(f"sf{i}", [128, T + 16], F32) for i in range(NSF)]
    r_sf = [P.res(f"sf{i}") for i in range(NSF)]
    NSB = 6
    sbf_t = [sb(f"sbf{i}", [128, T], BF16) for i in range(NSB)]
    r_sbf = [P.res(f"sbf{i}") for i in range(NSB)]
    colv = sb("colv", [128, NCOL], F32)
    r_colv = P.res("colv")
    b31 = sb("b31", [128, 4], F32)
    dbias = sb("dbias", [128, 4, 2, 128], F32)
    r_dbias = P.res("dbias")
    constk = sb("constk", [128, 34], F32)
    r_const = P.res("const")
    ones_bf = sb("ones", [128, 128], BF16)
    r_ones = P.res("ones")
    epsb = sb("epsb", [128, 2], F32)
    lamt = sb("lamt", [128, depth, 4], F32)
    r_lam = P.res("lam")
    gsub = sb("gsub", [128, depth], F32)
    lnrep = sb("lnrep", [128, 2, 256], F32)
    r_ln = P.res("ln")
    bsT = sb("bsT", [128, 2, 128], F32)
    r_bsT = P.res("bsT")
    wsT = sb("wsT", [128, 4, 128], BF16)
    r_wsT = P.res("wsT")
    poolbd = sb("poolbd", [128, 2, 128], BF16)
    r_poolbd = P.res("poolbd")
    carry = sb("carry", [128, NFF, 2], F32)
    r_carry = [P.res(f"carry{j}") for j in range(NFF)]
    small = sb("small", [128, 16], F32)
    r_small = P.res("small")
    bank = [nc.alloc_psum_tensor(f"bank{i}", [128, 512], F32) for i in range(8)]
    r_bank = [P.res(f"bank{i}") for i in range(8)]

    sf_i = [0]

    def sf():
        i = sf_i[0] % NSF
        sf_i[0] += 1
        return sf_t[i], r_sf[i]

    sb_i = [0]

    def sbf():
        i = sb_i[0] % NSB
        sb_i[0] += 1
        return sbf_t[i], r_sbf[i]

    ws_i = [0]

    def mm(out, lhsT, rhs, start, stop, reads, writes):
        P.op("pe", lambda e: e.matmul(out, lhsT=lhsT, rhs=rhs, start=start, stop=stop), reads, writes)

    def act(out, in_, func, reads, writes, bias=None, scale=1.0):
        if bias is None:
            P.op("act", lambda e: e.activation(out=out, in_=in_, func=func, scale=scale), reads, writes)
        else:
            P.op("act", lambda e: e.activation(out=out, in_=in_, func=func, bias=bias, scale=scale), reads, writes)

    def tt(out, in0, in1, op, reads, writes, eng="dve"):
        P.op(eng, lambda e: e.tensor_tensor(out=out, in0=in0, in1=in1, op=op), reads, writes)

    def ts(out, in0, s1, s2, op0, op1, reads, writes, eng="dve"):
        if s2 is None:
            P.op(eng, lambda e: e.tensor_scalar(out=out, in0=in0, scalar1=s1, scalar2=None, op0=op0), reads, writes)
        else:
            P.op(eng, lambda e: e.tensor_scalar(out=out, in0=in0, scalar1=s1, scalar2=s2, op0=op0, op1=op1), reads, writes)

    def stt(out, in0, scalar, in1, op0, op1, reads, writes, eng="dve"):
        P.op(eng, lambda e: e.scalar_tensor_tensor(out=out, in0=in0, scalar=scalar, in1=in1, op0=op0, op1=op1), reads, writes)

    def cp(out, in_, reads, writes, eng="dve"):
        P.op(eng, lambda e: e.tensor_copy(out=out, in_=in_), reads, writes)

    def recip(out, in_, reads, writes):
        P.op("dve", lambda e: e.reciprocal(out=out, in_=in_), reads, writes)

    def memset(ap, val, writes, eng="dve"):
        P.op(eng, lambda e: e.memset(ap, val), (), writes)

    def dma(eng, out, in_, reads, writes):
        P.op(eng, lambda e: e.dma_start(out=out, in_=in_), reads, writes, dma=True)

    def dump(name, ap, reads, shape, dt=F32):
        if not dbg:
            return
        d = nc.dram_tensor("dbg_" + name, list(shape), dt, kind="ExternalOutput").ap()
        r = P.res("dbg_" + name)
        dma("sp", d, ap, reads, [r])
        dbg_out[name] = r

    dma("sp", colv[:], colv_d, [], [r_colv])
    dma("sp", b31[:], b31_d, [], [r_const])
    dma("sp", constk[:], constk_d, [], [r_const])
    dma("sp", dbias[:], dbias_d, [], [r_dbias])
    mt, r_mt = sf()
    dma("sp", mt[:, 0:128], maskneg_d, [], [r_mt])
    for h in range(4):
        tt(dbias[:, h, 0, :], dbias[:, h, 0, :], mt[:, 0:128], ALU.add, [r_dbias, r_mt], [r_dbias])
    memset(ones_bf[:], 1.0, [r_ones])
    memset(epsb[:, 0:1], 1e-6, [r_const])
    memset(epsb[:, 1:2], 1e-5, [r_const])
    pr, r_pr = sf()
    for l in range(depth):
        L = layer0 + l
        lam_init = 0.8 - 0.6 * math.exp(-0.3 * L)
        lt, r_lt = sf()
        dma("sp", lt[:, 0:256], lamrep_d[:, l * 256:(l + 1) * 256], [], [r_lt])
        base = 0
        tt(pr[:, 0:64], lt[:, base:base + 64], lt[:, base + 64:base + 128], ALU.mult, [r_lt], [r_pr])
        tt(pr[:, 64:128], lt[:, base + 128:base + 192], lt[:, base + 192:base + 256], ALU.mult, [r_lt], [r_pr])
        P.op("dve", lambda e, l=l: e.reduce_sum(out=lamt[:, l, 2:3], in_=pr[:, 0:64], axis=mybir.AxisListType.X), [r_pr], [r_lam])
        P.op("dve", lambda e, l=l: e.reduce_sum(out=lamt[:, l, 3:4], in_=pr[:, 64:128], axis=mybir.AxisListType.X), [r_pr], [r_lam])
        act(lamt[:, l, 2:4], lamt[:, l, 2:4], AF.Exp, [r_lam], [r_lam])
        stt(lamt[:, l, 0:1], lamt[:, l, 3:4], -lam_init, lamt[:, l, 2:3], ALU.add, ALU.subtract, [r_lam], [r_lam])
        ts(gsub[:, l:l + 1], colv[:, l * NCOLV + 130:l * NCOLV + 131], 1.0 - lam_init, None, ALU.mult, None,
           [r_colv], [r_lam])

    r_cast = [P.res(f"cast{l}") for l in range(depth)]

    def cast_layer(l):
        def c2(dst, src, rows, step):
            for r0 in range(0, rows, step):
                r1 = min(rows, r0 + step)
                dma("pool", dst[r0:r1, :], src[r0:r1, :], [], [r_cast[l]])
        c2(win_bf[l], w_in_d[l], D, 256)
        c2(wgate_bf[l], w_gate_d[l], D, 256)
        c2(proj_bf[l, 0:256], proj_a_d[l], 256, 256)
        c2(proj_bf[l, 256:512], proj_b_d[l], 256, 256)
        c2(proj_bf[l, 512:1024], proj_c_d[l], 512, 256)
        c2(wout_bf[l], w_out_d[l], D, 256)
        c2(wup_bf[l], w_up_d[l], D, 128)
        c2(wdn_bf[l], w_down_d[l], DFF, 256)

    def load_blk(l, src, nk, mw):
        i = ws_i[0] % NSLOT
        ws_i[0] += 1
        dma("sp", wslot[i][:, 0:nk, 0:mw], src, [r_cast[l]], [r_wslot[i]])
        return wslot[i], r_wslot[i]

    def wview(t2d):
        return t2d.rearrange("(kc p) m -> p kc m", p=128)

    def rmsnorm_to_hT(l, gcol):
        nb = 7
        for kc in range(8):
            sq, r_sq = sbf()
            act(sq[:], xT[:, kc, :], AF.Square, [r_xT[kc]], [r_sq])
            mm(bank[nb][:], ones_bf[:], sq[:], kc == 0, kc == 7, [r_ones, r_sq], [r_bank[nb]])
        sd, r_sd = sf()
        act(sd[:, 0:T], bank[nb][:], AF.Sqrt, [r_bank[nb], r_const], [r_sd], bias=epsb[:, 0:1], scale=1.0 / D)
        recip(sd[:, 0:T], sd[:, 0:T], [r_sd], [r_sd])
        for kc in range(8):
            stt(hT[:, kc, :], xT[:, kc, :], colv[:, gcol + kc:gcol + kc + 1], sd[:, 0:T], ALU.mult, ALU.mult,
                [r_xT[kc], r_colv, r_sd], [r_hT[kc]])

    scale = 64 ** -0.5

    cast_layer(0)
    for l in range(depth):
        P.epoch = l
        cb = l * NCOLV
        if l + 1 < depth:
            cast_layer(l + 1)
        dma("sp", lnrep[:], lnrep_d[l], [], [r_ln])
        dma("sp", bsT[:], bsT_d[l], [], [r_bsT])
        wst, r_wst = sf()
        dma("sp", wst[:, 0:512], wsT_d[l].rearrange("p g t -> p (g t)"), [], [r_wst])
        trl, r_trl = sf()
        dma("sp", trl[:, 0:128], tril_d, [], [r_trl])
        for g in range(4):
            tt(wsT[:, g, :], wst[:, g * 128:(g + 1) * 128], trl[:, 0:128], ALU.mult, [r_wst, r_trl], [r_wsT])
        pbd, r_pbd = sf()
        dma("sp", pbd[:, 0:256], poolbd_d[l].rearrange("p c q -> p (c q)"), [], [r_pbd])
        cp(poolbd[:].rearrange("p c q -> p (c q)"), pbd[:, 0:256], [r_pbd], [r_poolbd])
        for j in range(NFF):
            memset(carry[:, j, :], 0.0, [r_carry[j]])
        for s4 in range(4):
            memset(vA[:, s4, :], 0.0, [r_vAB[s4]])
            memset(vB[:, s4, :], 0.0, [r_vAB[s4]])

        for it in range(NT):
            t0 = it * T
            for kc in range(8):
                pass
            if l == 0:
                src = xT_d.rearrange("(kc p) t -> p kc t", p=128)[:, :, t0:t0 + T]
                P.op("sp", lambda e, src=src: e.dma_start(out=xT[:], in_=src), [], [r_xT[0]], dma=True)
            else:
                P.op("sp", lambda e, it=it: e.dma_start(out=xT[:], in_=xs_d[it]), [r_xs], [r_xT[0]], dma=True)
            rmsnorm_to_hT(l, cb + 0)

            winv = wview(win_bf[l])
            wA, r_wA = load_blk(l, winv[:, :, 0:512], 8, 512)
            uT = [pb[:, 18:20, :].rearrange("p a t -> p (a t)").bitcast(F32),
                  pb[:, 20:22, :].rearrange("p a t -> p (a t)").bitcast(F32)]
            r_uT = [[r_pb[18], r_pb[19]], [r_pb[20], r_pb[21]]]
            for c in range(2):
                for kc in range(8):
                    mm(bank[c][:], wA[:, kc, c * 128:(c + 1) * 128], hT[:, kc, :], kc == 0, kc == 7,
                       [r_wA, r_hT[kc]], [r_bank[c]])
                act(uT[c], bank[c][:], AF.Gelu, [r_bank[c]], r_uT[c])
            for s4 in range(4):
                bk = 2 + (s4 % 2)
                for kc in range(8):
                    mm(bank[bk][:, 0:256], hT[:, kc, s4 * 128:(s4 + 1) * 128], wA[:, kc, 256:512], kc == 0, kc == 7,
                       [r_wA, r_hT[kc]], [r_bank[bk]])
                vg, r_vg = sf()
                act(vg[:, 0:256], bank[bk][:, 0:256], AF.Gelu, [r_bank[bk]], [r_vg])
                P.op("dve", lambda e, vg=vg: e.bn_stats(out=small[:, 0:6], in_=vg[:, 0:256]), [r_vg], [r_small])
                P.op("dve", lambda e: e.bn_aggr(out=small[:, 8:10], in_=small[:, 0:6]), [r_small], [r_small])
                act(small[:, 10:11], small[:, 9:10], AF.Sqrt, [r_small, r_const], [r_small], bias=epsb[:, 1:2])
                recip(small[:, 10:11], small[:, 10:11], [r_small], [r_small])
                ts(vg[:, 0:256], vg[:, 0:256], small[:, 8:9], small[:, 10:11], ALU.subtract, ALU.mult,
                   [r_vg, r_small], [r_vg])
                tt(vg[:, 0:256], vg[:, 0:256], lnrep[:, 0, :], ALU.mult, [r_vg, r_ln], [r_vg])

                def gview(ap2d, b):
                    return ap2d.rearrange("p (a b d) -> p a b d", a=2, b=2)[:, :, b, :]
                tt(gview(vA[:, s4, :], 0), gview(vg[:, 0:256], 0), gview(lnrep[:, 1, :], 0), ALU.add,
                   [r_vg, r_ln], [r_vAB[s4]])
                tt(gview(vB[:, s4, :], 1), gview(vg[:, 0:256], 1), gview(lnrep[:, 1, :], 1), ALU.add,
                   [r_vg, r_ln], [r_vAB[s4]])
            yaT = [pb[:, 8, :], pb[:, 9, :]]
            for c in range(2):
                bk = 4 + c
                for s4 in range(4):
                    mm(bank[bk][:, s4 * 128:(s4 + 1) * 128], vA[:, s4, c * 128:(c + 1) * 128], wsT[:, 2 * c, :],
                       True, False, [r_vAB[s4], r_wsT], [r_bank[bk]])
                    mm(bank[bk][:, s4 * 128:(s4 + 1) * 128], vB[:, s4, c * 128:(c + 1) * 128], wsT[:, 2 * c + 1, :],
                       False, True, [r_vAB[s4], r_wsT], [r_bank[bk]])
                tm, r_tm = sf()
                tt(tm[:, 0:T].rearrange("p (s t) -> p s t", s=4), bank[bk][:].rearrange("p (s t) -> p s t", s=4),
                   bsT[:, c, :].unsqueeze(1).broadcast_to([128, 4, 128]), ALU.add, [r_bank[bk], r_bsT], [r_tm])
                tt(yaT[c], tm[:, 0:T], uT[c], ALU.mult, [r_tm] + r_uT[c], [r_pb[8 + c]])
            if l == 0 and it == 0:
                dump("yaT", pb[:, 8:10, :], [r_pb[8], r_pb[9]], [128, 2, T], BF16)

            wB, r_wB = load_blk(l, winv[:, :, 512:768], 8, 256)
            pT = [pb[:, 16, :], pb[:, 17, :]]
            ybT = [pb[:, 10, :], pb[:, 11, :]]
            for c in range(2):
                bk = 6 + c
                for kc in range(8):
                    mm(bank[bk][:], wB[:, kc, c * 128:(c + 1) * 128], hT[:, kc, :], kc == 0, kc == 7,
                       [r_wB, r_hT[kc]], [r_bank[bk]])
                if it == 0:
                    memset(XB[:, c, 0:16], 0.0, [r_XB[c]])
                else:
                    cp(XB[:, c, 0:16], XB[:, c, T:T + 16], [r_XB[c]], [r_XB[c]])
                P.op("act", lambda e, c=c, bk=bk: e.activation(out=XB[:, c, 16:16 + T], in_=bank[bk][:], func=AF.Copy),
                     [r_bank[bk]], [r_XB[c]])
                A_, r_A = sf()
                B_, r_B = sf()
                W_ = T + 16
                x_ = XB[:, c, :]
                tt(A_[:, 1:W_], x_[:, 1:W_], x_[:, 0:W_ - 1], ALU.add, [r_XB[c]], [r_A])
                if c == 0:
                    tt(B_[64:128, 3:W_], A_[64:128, 3:W_], A_[64:128, 1:W_ - 2], ALU.add, [r_A], [r_B])
                    lo, hi = A_, B_
                else:
                    tt(B_[:, 3:W_], A_[:, 3:W_], A_[:, 1:W_ - 2], ALU.add, [r_A], [r_B])
                    tt(A_[:, 7:W_], B_[:, 7:W_], B_[:, 3:W_ - 4], ALU.add, [r_B, r_A], [r_A])
                    tt(B_[64:128, 15:W_], A_[64:128, 15:W_], A_[64:128, 7:W_ - 8], ALU.add, [r_A, r_B], [r_B])
                    lo, hi = A_, B_
                for half, wt_ in ((0, lo), (1, hi)):
                    ps = slice(half * 64, half * 64 + 64)
                    stt(pT[c][ps, :], wt_[ps, 16:W_], constk[ps, c:c + 1], x_[ps, 16:W_], ALU.mult, ALU.subtract,
                        [r_A, r_B, r_XB[c], r_const], [r_pb[16 + c]])
                    if it == 0:
                        f_, r_f = sf()
                        tt(f_[ps, 0:16], wt_[ps, 16:32], constk[ps, 2 + c * 16:2 + c * 16 + 16], ALU.mult,
                           [r_A, r_B, r_const], [r_f])
                        tt(pT[c][ps, 0:16], f_[ps, 0:16], x_[ps, 16:32], ALU.subtract, [r_f, r_XB[c]], [r_pb[16 + c]])
                bk2 = 2 + c
                mm(bank[bk2][:], poolbd[:, c, :], pT[c], True, True, [r_poolbd, r_pb[16 + c]], [r_bank[bk2]])
                ts(ybT[c], bank[bk2][:], colv[:, cb + 128 + c:cb + 129 + c], None, ALU.mult, None,
                   [r_bank[bk2], r_colv], [r_pb[10 + c]])
            if l == 0 and it == 0:
                dump("ybT", pb[:, 10:12, :], [r_pb[10], r_pb[11]], [128, 2, T], BF16)

            wQ, r_wQ = load_blk(l, winv[:, :, 768:1280], 8, 512)
            for h in range(4):
                bk = h % 4
                for kc in range(8):
                    mm(bank[bk][:], wQ[:, kc, h * 128:(h + 1) * 128], hT[:, kc, :], kc == 0, kc == 7,
                       [r_wQ, r_hT[kc]], [r_bank[bk]])
                cp(QT[:, h, :], bank[bk][:], [r_bank[bk]], [r_QT[h]])
            wK, r_wK = load_blk(l, winv[:, :, 1280:1792], 8, 512)
            for h in range(4):
                bk = 4 + h % 4
                for kc in range(8):
                    mm(bank[bk][:], wK[:, kc, h * 128:(h + 1) * 128], hT[:, kc, :], kc == 0, kc == 7,
                       [r_wK, r_hT[kc]], [r_bank[bk]])
                P.op("act", lambda e, h=h, bk=bk, t0=t0: e.activation(out=KT[:, h, t0:t0 + T], in_=bank[bk][:], func=AF.Copy),
                     [r_bank[bk]], [r_KT[h][it]])
            wV, r_wV = load_blk(l, winv[:, :, 1792:2304], 8, 512)
            for s4 in range(4):
                bk = s4 % 4
                kb = it * 4 + s4
                for kc in range(8):
                    mm(bank[bk][:], hT[:, kc, s4 * 128:(s4 + 1) * 128], wV[:, kc, :], kc == 0, kc == 7,
                       [r_wV, r_hT[kc]], [r_bank[bk]])
                cp(V[:, kb, :], bank[bk][:], [r_bank[bk]], [r_V[kb]])

            ycT = [pb[:, 12 + h, :] for h in range(4)]
            nkb = 4 * it + 4
            for h in range(4):
                SB = [0, 1, 7]
                OB = [2, 4]
                RB = [3, 5]
                pend = []
                sidx = 0

                def flush(pend):
                    for (c, kb, q0, PT, r_PT) in pend:
                        mm(bank[OB[c]][:, q0:T], V[:, kb, h * 128:(h + 1) * 128], PT[:, q0:T], kb == 0, kb == nkb - 1,
                           [r_V[kb], r_PT], [r_bank[OB[c]]])
                        mm(bank[RB[c]][:, q0:T], ones_bf[:], PT[:, q0:T], kb == 0, kb == nkb - 1,
                           [r_ones, r_PT], [r_bank[RB[c]]])

                for kb in range(nkb):
                    j = kb - 4 * it
                    q0 = 128 * max(0, j)
                    kit = kb // 4
                    newp = []
                    for c in range(2):
                        sbk = SB[sidx % 3]
                        sidx += 1
                        cs = slice(c * 64, c * 64 + 64)
                        mm(bank[sbk][:, q0:T], KT[cs, h, kb * 128:(kb + 1) * 128], QT[cs, h, q0:T], True, True,
                           [r_KT[h][kit], r_QT[h]], [r_bank[sbk]])
                        PT, r_PT = sbf()
                        if j < -1:
                            act(PT[:, q0:T], bank[sbk][:, q0:T], AF.Exp, [r_bank[sbk], r_const], [r_PT],
                                bias=b31[:, h:h + 1], scale=scale)
                        else:
                            tmp, r_tmp = sf()
                            for jj in range(max(0, j), 4):
                                qs = slice(jj * 128, jj * 128 + 128)
                                if jj == j:
                                    stt(tmp[:, qs], bank[sbk][:, qs], scale, dbias[:, h, 0, :], ALU.mult, ALU.add,
                                        [r_bank[sbk], r_dbias], [r_tmp])
                                elif jj == j + 1:
                                    stt(tmp[:, qs], bank[sbk][:, qs], scale, dbias[:, h, 1, :], ALU.mult, ALU.add,
                                        [r_bank[sbk], r_dbias], [r_tmp])
                                else:
                                    qr = slice(jj * 128, T)
                                    ts(tmp[:, qr], bank[sbk][:, qr], scale, b31[:, h:h + 1], ALU.mult, ALU.add,
                                       [r_bank[sbk], r_const], [r_tmp])
                                    break
                            act(PT[:, q0:T], tmp[:, q0:T], AF.Exp, [r_tmp], [r_PT])
                        newp.append((c, kb, q0, PT, r_PT))
                    flush(pend)
                    pend = newp
                flush(pend)
                i0, r_i0 = sf()
                i1, r_i1 = sf()
                recip(i0[:, 0:T], bank[RB[0]][:], [r_bank[RB[0]]], [r_i0])
                recip(i1[:, 0:T], bank[RB[1]][:], [r_bank[RB[1]]], [r_i1])
                tt(i0[:, 0:T], bank[OB[0]][:], i0[:, 0:T], ALU.mult, [r_bank[OB[0]], r_i0], [r_i0])
                tt(i1[:, 0:T], bank[OB[1]][:], i1[:, 0:T], ALU.mult, [r_bank[OB[1]], r_i1], [r_i1])
                stt(i0[:, 0:T], i1[:, 0:T], lamt[:, l, 0:1], i0[:, 0:T], ALU.mult, ALU.add, [r_i0, r_i1, r_lam], [r_i0])
                sq, r_sq = sbf()
                act(sq[:], i0[:, 0:T], AF.Square, [r_i0], [r_sq])
                mm(bank[6][:], ones_bf[:], sq[:], True, True, [r_ones, r_sq], [r_bank[6]])
                act(i1[:, 0:T], bank[6][:], AF.Sqrt, [r_bank[6], r_const], [r_i1], bias=epsb[:, 1:2], scale=1.0 / 128)
                recip(i1[:, 0:T], i1[:, 0:T], [r_i1], [r_i1])
                stt(ycT[h], i0[:, 0:T], gsub[:, l:l + 1], i1[:, 0:T], ALU.mult, ALU.mult, [r_i0, r_i1, r_lam],
                    [r_pb[12 + h]])
            if l == 0 and it == 0:
                dump("ycT", pb[:, 12:16, :], [r_pb[12 + h] for h in range(4)], [128, 4, T], BF16)

            wgv = wview(wgate_bf[l])
            prv = wview(proj_bf[l])
            for half in range(2):
                wg = []
                for br in range(3):
                    c0 = br * 1024 + half * 512
                    wg.append(load_blk(l, wgv[:, :, c0:c0 + 512], 8, 512))
                wp, r_wp = load_blk(l, prv[:, :, half * 512:half * 512 + 512], 8, 512)
                for ml in range(4):
                    m = half * 4 + ml
                    cs = slice(ml * 128, ml * 128 + 128)
                    sg = []
                    for br in range(3):
                        for kc in range(8):
                            mm(bank[br][:], wg[br][0][:, kc, cs], hT[:, kc, :], kc == 0, kc == 7,
                               [wg[br][1], r_hT[kc]], [r_bank[br]])
                        s_, r_s = sf()
                        act(s_[:, 0:T], bank[br][:], AF.Sigmoid, [r_bank[br], r_colv], [r_s],
                            bias=colv[:, cb + 16 + br * 8 + m:cb + 17 + br * 8 + m])
                        sg.append((s_, r_s))
                    srcs = [([pb[:, 8, :], pb[:, 9, :]], [r_pb[8], r_pb[9]], 0),
                            ([pb[:, 10, :], pb[:, 11, :]], [r_pb[10], r_pb[11]], 2),
                            ([pb[:, 12 + h, :] for h in range(4)], [r_pb[12 + h] for h in range(4)], 4)]
                    for br in range(3):
                        ys, rys, k0 = srcs[br]
                        for i_, (y_, ry_) in enumerate(zip(ys, rys)):
                            mm(bank[3 + br][:], wp[:, k0 + i_, cs], y_, i_ == 0, i_ == len(ys) - 1,
                               [r_wp, ry_], [r_bank[3 + br]])
                        s_, r_s = sg[br]
                        tt(s_[:, 0:T], bank[3 + br][:], s_[:, 0:T], ALU.mult, [r_bank[3 + br], r_s], [r_s])
                    tt(sg[0][0][:, 0:T], sg[0][0][:, 0:T], sg[1][0][:, 0:T], ALU.add, [sg[0][1], sg[1][1]], [sg[0][1]])
                    tt(pb[:, m, :], sg[0][0][:, 0:T], sg[2][0][:, 0:T], ALU.add, [sg[0][1], sg[2][1]], [r_pb[m]])
            if l == 0 and it == 0:
                dump("mergedT", pb[:, 0:8, :], [r_pb[m] for m in range(8)], [128, 8, T], BF16)

            wov = wview(wout_bf[l])
            for half in range(2):
                wo, r_wo = load_blk(l, wov[:, :, half * 512:half * 512 + 512], 8, 512)
                for ml in range(4):
                    m = half * 4 + ml
                    bk = 6 + (ml % 2)
                    for kc in range(8):
                        mm(bank[bk][:], wo[:, kc, ml * 128:(ml + 1) * 128], pb[:, kc, :], kc == 0, kc == 7,
                           [r_wo, r_pb[kc]], [r_bank[bk]])
                    tt(xT[:, m, :], xT[:, m, :], bank[bk][:], ALU.add, [r_xT[m], r_bank[bk]], [r_xT[m]])
            if l == 0 and it == 0:
                dump("x1T", xT[:], [r_xTall], [128, 8, T], F32)

            rmsnorm_to_hT(l, cb + 8)
            wuv = wview(wup_bf[l])
            for jb in range(6):
                mw = 512 if jb < 5 else 256
                wG, r_wG = load_blk(l, wuv[:, :, jb * 512:jb * 512 + mw], 8, mw)
                wU, r_wU = load_blk(l, wuv[:, :, DFF + jb * 512:DFF + jb * 512 + mw], 8, mw)
                for jj in range(mw // 128):
                    j = jb * 4 + jj
                    gb = j % 2
                    ub = 2 + j % 2
                    cs = slice(jj * 128, jj * 128 + 128)
                    for kc in range(8):
                        mm(bank[gb][:], wG[:, kc, cs], hT[:, kc, :], kc == 0, kc == 7, [r_wG, r_hT[kc]], [r_bank[gb]])
                    for kc in range(8):
                        mm(bank[ub][:], wU[:, kc, cs], hT[:, kc, :], kc == 0, kc == 7, [r_wU, r_hT[kc]], [r_bank[ub]])
                    G_, r_G = sf()
                    cp(G_[:, 0:2], carry[:, j, :], [r_carry[j]], [r_G])
                    P.op("act", lambda e, G_=G_, gb=gb: e.activation(out=G_[:, 2:2 + T], in_=bank[gb][:], func=AF.Copy),
                         [r_bank[gb]], [r_G])
                    cp(carry[:, j, :], G_[:, T:T + 2], [r_G], [r_carry[j]])
                    a_, r_a = sf()
                    cw = cb + 40
                    ts(a_[:, 0:T], G_[:, 0:T], colv[:, cw + j:cw + j + 1], colv[:, cb + 106 + j:cb + 107 + j],
                       ALU.mult, ALU.add, [r_G, r_colv], [r_a])
                    stt(a_[:, 0:T], G_[:, 1:T + 1], colv[:, cw + 22 + j:cw + 23 + j], a_[:, 0:T], ALU.mult, ALU.add,
                        [r_G, r_a, r_colv], [r_a])
                    stt(a_[:, 0:T], G_[:, 2:T + 2], colv[:, cw + 44 + j:cw + 45 + j], a_[:, 0:T], ALU.mult, ALU.add,
                        [r_G, r_a, r_colv], [r_a])
                    act(a_[:, 0:T], a_[:, 0:T], AF.Gelu, [r_a], [r_a])
                    tt(pb[:, j, :], a_[:, 0:T], bank[ub][:], ALU.mult, [r_a, r_bank[ub]], [r_pb[j]])
            wdv = wview(wdn_bf[l])
            for mb in range(2):
                for kt in range(3):
                    nk = 8 if kt < 2 else 6
                    wd, r_wd = load_blk(l, wdv[:, kt * 8:kt * 8 + nk, mb * 512:mb * 512 + 512], nk, 512)
                    for ml in range(4):
                        bk = 4 + ml
                        for kk in range(nk):
                            kc = kt * 8 + kk
                            mm(bank[bk][:], wd[:, kk, ml * 128:(ml + 1) * 128], pb[:, kc, :], kc == 0, kc == NFF - 1,
                               [r_wd, r_pb[kc]], [r_bank[bk]])
                for ml in range(4):
                    m = mb * 4 + ml
                    tt(xT[:, m, :], xT[:, m, :], bank[4 + ml][:], ALU.add, [r_xT[m], r_bank[4 + ml]], [r_xT[m]])

            if l < depth - 1:
                P.op("sp", lambda e, it=it: e.dma_start(out=xs_d[it], in_=xT[:]), [r_xTall], [r_xs], dma=True)
            else:
                gcol = depth * NCOLV
                nb = 7
                for kc in range(8):
                    sq, r_sq = sbf()
                    act(sq[:], xT[:, kc, :], AF.Square, [r_xT[kc]], [r_sq])
                    mm(bank[nb][:], ones_bf[:], sq[:], kc == 0, kc == 7, [r_ones, r_sq], [r_bank[nb]])
                sd, r_sd = sf()
                act(sd[:, 0:T], bank[nb][:], AF.Sqrt, [r_bank[nb], r_const], [r_sd], bias=epsb[:, 0:1], scale=1.0 / D)
                recip(sd[:, 0:T], sd[:, 0:T], [r_sd], [r_sd])
                for kc in range(8):
                    stt(xT[:, kc, :], xT[:, kc, :], colv[:, gcol + kc:gcol + kc + 1], sd[:, 0:T], ALU.mult, ALU.mult,
                        [r_xT[kc], r_colv, r_sd], [r_xT[kc]])
                dst = outT_d.rearrange("(kc p) t -> p kc t", p=128)[:, :, t0:t0 + T]
                P.op("sp", lambda e, dst=dst: e.dma_start(out=dst, in_=xT[:]), [r_xTall], [r_out], dma=True)

    P.epoch = depth
    fin = [r_out] + list(dbg_out.values())
    P.op("sp", lambda e: None, fin, [])
    P.emit(st)
    st.close()
    return nc, P


def _t5_bucket_np(rel):
    n = np.maximum(rel, 0)
    nf = np.maximum(n, 1).astype(np.float32)
    large = 16 + (np.log(nf / np.float32(16)) / np.float32(math.log(128 / 16)) * np.float32(16)).astype(np.int32)
    large = np.minimum(large, 31)
    return np.where(n < 16, n, large)


def _bucket_tables():
    k = np.arange(128)[:, None]
    q = np.arange(128)[None, :]
    try:
        import jax
        import jax.numpy as jnp

        def bk(rel):
            nn = jnp.maximum(rel, 0)
            nf = jnp.maximum(nn, 1).astype(jnp.float32)
            large = 16 + (jnp.log(nf / 16) / math.log(128 / 16) * 16).astype(jnp.int32)
            large = jnp.minimum(large, 31)
            return np.asarray(jnp.where(nn < 16, nn, large))
        with jax.default_device(jax.devices("cpu")[0]):
            b0 = bk(jnp.asarray(q - k))
            b1 = bk(jnp.asarray(128 + q - k))
        return b0, b1
    except Exception:
        return _t5_bucket_np(q - k), _t5_bucket_np(128 + q - k)


def prep_inputs(inp, depth, b, S):
    f = np.float32
    m = {}
    m["xT"] = np.ascontiguousarray(inp["x"][b, :S].T)
    for k in ("w_in", "w_gate", "proj_a", "proj_b", "proj_c", "w_out", "w_up", "w_down"):
        m[k] = np.ascontiguousarray(inp[k][:depth])
    NCOL = depth * NCOLV + 8
    colv = np.zeros((128, NCOL), f)
    for l in range(depth):
        cb = l * NCOLV
        colv[:, cb + 0:cb + 8] = inp["attn_norm_g"][l].reshape(8, 128).T
        colv[:, cb + 8:cb + 16] = inp["ffn_norm_g"][l].reshape(8, 128).T
        colv[:, cb + 16:cb + 40] = inp["b_gate"][l].reshape(24, 128).T
        for tap in range(3):
            colv[:, cb + 40 + tap * 22:cb + 62 + tap * 22] = inp["conv_w"][l, tap].reshape(22, 128).T
        colv[:, cb + 106:cb + 128] = inp["conv_b"][l].reshape(22, 128).T
        colv[:, cb + 128:cb + 130] = inp["pool_scale"][l].reshape(2, 128).T
        colv[:, cb + 130] = inp["diff_subln_g"][l]
    colv[:, depth * NCOLV:depth * NCOLV + 8] = inp["final_norm_g"].reshape(8, 128).T
    m["colv"] = colv
    m["lamrep"] = np.ascontiguousarray(np.broadcast_to(inp["diff_lam"][:depth].reshape(1, depth * 256), (128, depth * 256)))
    m["b31rep"] = np.ascontiguousarray(np.broadcast_to(inp["rel_bias"][31][None, :], (128, 4)))
    b0, b1 = _bucket_tables()
    rb = inp["rel_bias"]
    dbias = np.zeros((128, 4, 2, 128), f)
    for h in range(4):
        dbias[:, h, 0, :] = rb[b0, h]
        dbias[:, h, 1, :] = rb[b1, h]
    m["dbias"] = dbias
    k = np.arange(128)[:, None]
    q = np.arange(128)[None, :]
    m["maskneg"] = np.where(q >= k, 0.0, -30000.0).astype(f)
    m["tril01"] = np.where(q >= k, 1.0, 0.0).astype(f)
    ln = np.zeros((depth, 128, 2, 256), f)
    ln[:, :, 0, :] = inp["sgu_ln_g"][:depth, None, :]
    ln[:, :, 1, :] = inp["sgu_ln_b"][:depth, None, :]
    m["lnrep"] = ln
    bs = np.zeros((depth, 128, 2, 128), f)
    for c in range(2):
        for hf in range(2):
            bs[:, hf * 64:(hf + 1) * 64, c, :] = inp["sgu_b"][:depth, 2 * c + hf][:, None, :]
    m["bsT"] = bs
    m["wsT"] = np.ascontiguousarray(np.transpose(inp["sgu_w"][:depth], (0, 3, 1, 2)))
    pbd = np.zeros((depth, 128, 2, 128), f)
    for c in range(2):
        for hf in range(2):
            pbd[:, hf * 64:(hf + 1) * 64, c, hf * 64:(hf + 1) * 64] = inp["pool_w"][:depth, 2 * c + hf]
    m["poolbd"] = pbd
    ck = np.zeros((128, 34), f)
    wins = [[2, 4], [8, 16]]
    for c in range(2):
        for hf in range(2):
            w = wins[c][hf]
            ck[hf * 64:(hf + 1) * 64, c] = 1.0 / w
            ck[hf * 64:(hf + 1) * 64, 2 + c * 16:2 + c * 16 + 16] = 1.0 / np.minimum(np.arange(16) + 1, w)
    m["constk"] = ck
    return m


_CACHE = {}


def kernel(**inputs):
    inp = {k: np.asarray(v) for k, v in inputs.items()}
    B, S, _ = inp["x"].shape
    depth = inp["w_in"].shape[0]
    key = (S, depth)
    if key not in _CACHE:
        _CACHE[key] = build(S=S, depth=depth)[0]
    nc = _CACHE[key]
    in_maps = [prep_inputs(inp, depth, b, S) for b in range(B)]
    res = run_bass_kernel_spmd(nc, in_maps, core_ids=list(range(B)))
    out = np.stack([np.asarray(r["outT"]).T for r in res.results], axis=0)
    return np.ascontiguousarray(out.astype(np.float32))
```

```python
import math
from contextlib import ExitStack

import numpy as np
import concourse.bass as bass
import concourse.mybir as mybir
from concourse.bass_utils import run_bass_kernel_spmd

F32 = mybir.dt.float32
BF16 = mybir.dt.bfloat16
AF = mybir.ActivationFunctionType
ALU = mybir.AluOpType

D = 1024
DFF = 2816
NFF = DFF // 128
T = 512
NCOLV = 131
PAIR_EXP = False
POOL_SQ = True


class Res:
    __slots__ = ("name", "lw", "rd", "chan")

    def __init__(self, name):
        self.name = name
        self.lw = None
        self.rd = []
        self.chan = None


class Ins:
    __slots__ = ("eng", "fn", "deps", "sig", "val", "is_dma", "chan", "epoch")


class Prog:
    ENGS = ("pe", "act", "dve", "pool", "sp")

    def __init__(self, nc, n_epochs=1, same_engine_sync=True):
        self.nc = nc
        self.q = {e: [] for e in self.ENGS}
        self.n_epochs = n_epochs
        self.epoch = 0
        self.same_engine_sync = same_engine_sync
        self.chans = []

    def res(self, name):
        return Res(name)

    def op(self, eng, fn, reads=(), writes=(), dma=False):
        ins = Ins()
        ins.eng = eng
        ins.fn = fn
        ins.is_dma = dma
        ins.epoch = self.epoch
        ins.sig = False
        ins.val = None
        ins.chan = None
        deps = set()
        for r in reads:
            if r.lw is not None:
                deps.add(r.lw)
        for w in writes:
            if w.lw is not None:
                deps.add(w.lw)
            last = {}
            for rdr in w.rd:
                if rdr.is_dma:
                    deps.add(rdr)
                else:
                    last[rdr.eng] = rdr
            deps.update(last.values())
        deps.discard(ins)
        for r in reads:
            r.rd.append(ins)
        for w in writes:
            w.lw = ins
            w.rd = []
        if dma:
            assert len(writes) == 1
            w = writes[0]
            if w.chan is None:
                w.chan = {"sem": None, "count": 0, "name": w.name}
                self.chans.append(w.chan)
            ins.chan = w.chan
        pruned = set()
        for d in deps:
            if d.is_dma:
                pruned.add(d)
            elif d.eng == eng:
                if eng in ("pe", "sp"):
                    continue
                if dma or self.same_engine_sync:
                    pruned.add(d)
            else:
                pruned.add(d)
        ins.deps = pruned
        for d in pruned:
            d.sig = True
        self.q[eng].append(ins)
        return ins

    def emit(self, stack):
        nc = self.nc
        esem = {}
        for e in self.ENGS:
            for ep in range(self.n_epochs):
                esem[(e, ep)] = stack.enter_context(nc.semaphore(f"s_{e}_{ep}"))
        for i, ch in enumerate(self.chans):
            ch["sem"] = stack.enter_context(nc.semaphore(f"d_{i}"))
        cnt = {}
        for e in self.ENGS:
            for ins in self.q[e]:
                if ins.is_dma:
                    ins.chan["count"] += 16
                    ins.val = ins.chan["count"]
                elif ins.sig:
                    k = (e, ins.epoch)
                    cnt[k] = cnt.get(k, 0) + 1
                    ins.val = cnt[k]
        self.stats = {e: len(self.q[e]) for e in self.ENGS}
        self.stats["cnt"] = {f"{k[0]}{k[1]}": v for k, v in cnt.items()}
        block = stack.enter_context(nc.Block())

        def run(e, engobj):
            waited = {}
            nw = 0
            for ins in self.q[e]:
                need = {}
                for d in ins.deps:
                    if d.is_dma:
                        key = ("dma", id(d.chan))
                        sem = d.chan["sem"]
                    else:
                        key = (d.eng, d.epoch)
                        sem = esem[key]
                    v = d.val
                    if waited.get(key, 0) >= v:
                        continue
                    if key not in need or need[key][1] < v:
                        need[key] = (sem, v)
                for key, (sem, v) in need.items():
                    engobj.wait_ge(sem, v)
                    waited[key] = v
                    nw += 1
                bi = ins.fn(engobj)
                if bi is None:
                    continue
                if ins.is_dma:
                    bi.then_inc(ins.chan["sem"], 16)
                elif ins.sig:
                    bi.then_inc(esem[(e, ins.epoch)], 1)
            self.stats["waits_" + e] = nw

        block.sync(lambda eng: run("sp", eng))
        block.tensor(lambda eng: run("pe", eng))
        block.scalar(lambda eng: run("act", eng))
        block.vector(lambda eng: run("dve", eng))
        block.gpsimd(lambda eng: run("pool", eng))


def build(S=4096, depth=4, dbg=False, layer0=0):
    NT = S // T
    nc = bass.Bass("TRN2", target_bir_lowering=False)
    P = Prog(nc, n_epochs=depth + 1)
    st = ExitStack()

    def din(name, shape, dt=F32):
        return nc.dram_tensor(name, list(shape), dt, kind="ExternalInput").ap()

    xT_d = din("xT", [D, S])
    w_in_d = din("w_in", [depth, D, 2304])
    w_gate_d = din("w_gate", [depth, D, 3072])
    proj_a_d = din("proj_a", [depth, 256, D])
    proj_b_d = din("proj_b", [depth, 256, D])
    proj_c_d = din("proj_c", [depth, 512, D])
    w_out_d = din("w_out", [depth, D, D])
    w_up_d = din("w_up", [depth, D, 2 * DFF])
    w_down_d = din("w_down", [depth, DFF, D])
    NCOL = depth * NCOLV + 8
    colv_d = din("colv", [128, NCOL])
    lamrep_d = din("lamrep", [128, depth * 256])
    b31_d = din("b31rep", [128, 4])
    dbias_d = din("dbias", [128, 4, 2, 128])
    maskneg_d = din("maskneg", [128, 128])
    tril_d = din("tril01", [128, 128])
    lnrep_d = din("lnrep", [depth, 128, 2, 256])
    bsT_d = din("bsT", [depth, 128, 2, 128])
    wsT_d = din("wsT", [depth, 128, 4, 128])
    poolbd_d = din("poolbd", [depth, 128, 2, 128])
    constk_d = din("constk", [128, 34])
    outT_d = nc.dram_tensor("outT", [D, S], F32, kind="ExternalOutput").ap()

    win_bf = nc.dram_tensor("win_bf", [depth, D, 2304], BF16).ap()
    wgate_bf = nc.dram_tensor("wgate_bf", [depth, D, 3072], BF16).ap()
    proj_bf = nc.dram_tensor("proj_bf", [depth, D, D], BF16).ap()
    wout_bf = nc.dram_tensor("wout_bf", [depth, D, D], BF16).ap()
    wup_bf = nc.dram_tensor("wup_bf", [depth, D, 2 * DFF], BF16).ap()
    wdn_bf = nc.dram_tensor("wdn_bf", [depth, DFF, D], BF16).ap()
    xs_d = nc.dram_tensor("xs", [NT, 128, 8, T], F32).ap()
    r_xs = [P.res(f"xs{k}") for k in range(8)]
    r_out = P.res("out")

    dbg_out = {}

    def sb(name, shape, dt):
        return nc.alloc_sbuf_tensor("s_" + name, list(shape), dt)

    xT = sb("xTt", [128, 8, T], F32)
    r_xT = [P.res(f"xT{k}") for k in range(8)]
    hT = sb("hT", [128, 8, T], BF16)
    r_hT = [P.res(f"hT{k}") for k in range(8)]
    KT = sb("KT", [128, 4, S], BF16)
    r_KT = [[P.res(f"KT{h}_{i}") for i in range(NT)] for h in range(4)]
    V = sb("V", [128, S // 128, 512], BF16)
    r_V = [P.res(f"V{kb}") for kb in range(S // 128)]
    QT = sb("QT", [128, 4, T], BF16)
    r_QT = [P.res(f"QT{h}") for h in range(4)]
    vA = sb("vA", [128, 4, 256], BF16)
    vB = sb("vB", [128, 4, 256], BF16)
    r_vAB = [P.res(f"vAB{s}") for s in range(4)]
    XB = sb("XB", [128, 2, 16 + T], F32)
    r_XB = [P.res(f"XB{c}") for c in range(2)]
    pb = sb("pb", [128, NFF, T], BF16)
    r_pb = [P.res(f"pb{j}") for j in range(NFF)]
    NSLOT = 5
    wslot = [sb(f"wslot{i}", [128, 8, 512], BF16) for i in range(NSLOT)]
    r_wslot = [P.res(f"wslot{i}") for i in range(NSLOT)]
    NSF = 8
    sf_t = [sb(f"sf{i}", [128, T + 16], F32) for i in range(NSF)]
    r_sf = [P.res(f"sf{i}") for i in range(NSF)]
    NSB = 4
    sbf_t = [sb(f"sbf{i}", [128, T], BF16) for i in range(NSB)]
    r_sbf = [P.res(f"sbf{i}") for i in range(NSB)]
    colv = sb("colv", [128, NCOL], F32)
    r_colv = P.res("colv")
    b31 = sb("b31", [128, 4], F32)
    bcat = sb("bcat", [128, 4, 640], F32)
    r_dbias = P.res("bcat")
    constk = sb("constk", [128, 34], F32)
    r_const = P.res("const")
    ones_bf = sb("ones", [128, 128], BF16)
    r_ones = P.res("ones")
    epsb = sb("epsb", [128, 2], F32)
    lamt = sb("lamt", [128, depth, 4], F32)
    r_lam = P.res("lam")
    gsub = sb("gsub", [128, depth], F32)
    lnrep = sb("lnrep", [128, 2, 256], F32)
    r_ln = P.res("ln")
    bsT = sb("bsT", [128, 2, 128], F32)
    r_bsT = P.res("bsT")
    wsT = sb("wsT", [128, 4, 128], BF16)
    r_wsT = P.res("wsT")
    poolbd = sb("poolbd", [128, 2, 128], BF16)
    r_poolbd = P.res("poolbd")
    carry = sb("carry", [128, NFF, 2], F32)
    r_carry = [P.res(f"carry{j}") for j in range(NFF)]
    small = sb("small", [128, 16], F32)
    r_small = P.res("small")
    ps_all = nc.alloc_psum_tensor("ps_all", [128, 8, 512], F32)
    bank = [ps_all[:, i, :] for i in range(8)]
    ptb = [sb(f"ptb{i}", [128, 2, T], BF16) for i in range(3)]
    r_ptb = [P.res(f"ptb{i}") for i in range(3)]
    small2 = sb("small2", [128, 8], F32)
    r_bank = [P.res(f"bank{i}") for i in range(8)]

    sf_i = [0]

    def sf():
        i = sf_i[0] % NSF
        sf_i[0] += 1
        return sf_t[i], r_sf[i]

    sb_i = [0]

    def sbf():
        i = sb_i[0] % NSB
        sb_i[0] += 1
        return sbf_t[i], r_sbf[i]

    ws_i = [0]

    def mm(out, lhsT, rhs, start, stop, reads, writes):
        P.op("pe", lambda e: e.matmul(out, lhsT=lhsT, rhs=rhs, start=start, stop=stop), reads, writes)

    def act(out, in_, func, reads, writes, bias=None, scale=1.0):
        if bias is None:
            P.op("act", lambda e: e.activation(out=out, in_=in_, func=func, scale=scale), reads, writes)
        else:
            P.op("act", lambda e: e.activation(out=out, in_=in_, func=func, bias=bias, scale=scale), reads, writes)

    def tt(out, in0, in1, op, reads, writes, eng="dve"):
        P.op(eng, lambda e: e.tensor_tensor(out=out, in0=in0, in1=in1, op=op), reads, writes)

    def ts(out, in0, s1, s2, op0, op1, reads, writes, eng="dve"):
        if s2 is None:
            P.op(eng, lambda e: e.tensor_scalar(out=out, in0=in0, scalar1=s1, scalar2=None, op0=op0), reads, writes)
        else:
            P.op(eng, lambda e: e.tensor_scalar(out=out, in0=in0, scalar1=s1, scalar2=s2, op0=op0, op1=op1), reads, writes)

    def stt(out, in0, scalar, in1, op0, op1, reads, writes, eng="dve"):
        P.op(eng, lambda e: e.scalar_tensor_tensor(out=out, in0=in0, scalar=scalar, in1=in1, op0=op0, op1=op1), reads, writes)

    def cp(out, in_, reads, writes, eng="dve"):
        P.op(eng, lambda e: e.tensor_copy(out=out, in_=in_), reads, writes)

    def recip(out, in_, reads, writes):
        P.op("dve", lambda e: e.reciprocal(out=out, in_=in_), reads, writes)

    def memset(ap, val, writes, eng="dve"):
        P.op(eng, lambda e: e.memset(ap, val), (), writes)

    def dma(eng, out, in_, reads, writes):
        P.op(eng, lambda e: e.dma_start(out=out, in_=in_), reads, writes, dma=True)

    def dump(name, ap, reads, shape, dt=F32):
        if not dbg:
            return
        d = nc.dram_tensor("dbg_" + name, list(shape), dt, kind="ExternalOutput").ap()
        r = P.res("dbg_" + name)
        dma("sp", d, ap, reads, [r])
        dbg_out[name] = r

    dma("sp", colv[:], colv_d, [], [r_colv])
    dma("sp", b31[:], b31_d, [], [r_const])
    dma("sp", constk[:], constk_d, [], [r_const])
    mt, r_mt = sf()
    dma("sp", mt[:, 0:128], maskneg_d, [], [r_mt])
    for h in range(4):
        dbt, r_dbt = sf()
        dma("sp", dbt[:, 0:256], dbias_d[:, h].rearrange("p a q -> p (a q)"), [], [r_dbt])
        tt(bcat[:, h, 0:128], dbt[:, 0:128], mt[:, 0:128], ALU.add, [r_dbt, r_mt], [r_dbias])
        cp(bcat[:, h, 128:256], dbt[:, 128:256], [r_dbt], [r_dbias])
        memset(bcat[:, h, 256:640], 0.0, [r_dbias])
        ts(bcat[:, h, 256:640], bcat[:, h, 256:640], b31[:, h:h + 1], None, ALU.add, None, [r_dbias, r_const], [r_dbias])
    memset(ones_bf[:], 1.0, [r_ones])
    memset(epsb[:, 0:1], 1e-6, [r_const])
    memset(epsb[:, 1:2], 1e-5, [r_const])
    pr, r_pr = sf()
    for l in range(depth):
        L = layer0 + l
        lam_init = 0.8 - 0.6 * math.exp(-0.3 * L)
        lt, r_lt = sf()
        dma("sp", lt[:, 0:256], lamrep_d[:, l * 256:(l + 1) * 256], [], [r_lt])
        base = 0
        tt(pr[:, 0:64], lt[:, base:base + 64], lt[:, base + 64:base + 128], ALU.mult, [r_lt], [r_pr])
        tt(pr[:, 64:128], lt[:, base + 128:base + 192], lt[:, base + 192:base + 256], ALU.mult, [r_lt], [r_pr])
        P.op("dve", lambda e, l=l: e.reduce_sum(out=lamt[:, l, 2:3], in_=pr[:, 0:64], axis=mybir.AxisListType.X), [r_pr], [r_lam])
        P.op("dve", lambda e, l=l: e.reduce_sum(out=lamt[:, l, 3:4], in_=pr[:, 64:128], axis=mybir.AxisListType.X), [r_pr], [r_lam])
        act(lamt[:, l, 2:4], lamt[:, l, 2:4], AF.Exp, [r_lam], [r_lam])
        stt(lamt[:, l, 0:1], lamt[:, l, 3:4], -lam_init, lamt[:, l, 2:3], ALU.add, ALU.subtract, [r_lam], [r_lam])
        ts(gsub[:, l:l + 1], colv[:, l * NCOLV + 130:l * NCOLV + 131], 1.0 - lam_init, None, ALU.mult, None,
           [r_colv], [r_lam])

    r_cast = [P.res(f"cast{l}") for l in range(depth)]

    def cast_layer(l):
        def c2(dst, src, rows, step):
            for r0 in range(0, rows, step):
                r1 = min(rows, r0 + step)
                dma("pool", dst[r0:r1, :], src[r0:r1, :], [], [r_cast[l]])
        c2(win_bf[l], w_in_d[l], D, 256)
        c2(wgate_bf[l], w_gate_d[l], D, 256)
        c2(proj_bf[l, 0:256], proj_a_d[l], 256, 256)
        c2(proj_bf[l, 256:512], proj_b_d[l], 256, 256)
        c2(proj_bf[l, 512:1024], proj_c_d[l], 512, 256)
        c2(wout_bf[l], w_out_d[l], D, 256)
        c2(wup_bf[l], w_up_d[l], D, 128)
        c2(wdn_bf[l], w_down_d[l], DFF, 256)

    def load_blk(l, src, nk, mw):
        i = ws_i[0] % NSLOT
        ws_i[0] += 1
        dma("sp", wslot[i][:, 0:nk, 0:mw], src, [r_cast[l]], [r_wslot[i]])
        return wslot[i], r_wslot[i]

    def wview(t2d):
        return t2d.rearrange("(kc p) m -> p kc m", p=128)

    def rmsnorm_to_hT(l, gcol):
        nb = 7
        for kc in range(8):
            sq, r_sq = sbf()
            act(sq[:], xT[:, kc, :], AF.Square, [r_xT[kc]], [r_sq])
            mm(bank[nb][:], ones_bf[:], sq[:], kc == 0, kc == 7, [r_ones, r_sq], [r_bank[nb]])
        sd, r_sd = sf()
        act(sd[:, 0:T], bank[nb][:], AF.Ln, [r_bank[nb], r_const], [r_sd], bias=epsb[:, 0:1], scale=1.0 / D)
        act(sd[:, 0:T], sd[:, 0:T], AF.Exp, [r_sd], [r_sd], scale=-0.5)
        for kc in range(8):
            stt(hT[:, kc, :], xT[:, kc, :], colv[:, gcol + kc:gcol + kc + 1], sd[:, 0:T], ALU.mult, ALU.mult,
                [r_xT[kc], r_colv, r_sd], [r_hT[kc]])

    scale = 64 ** -0.5

    cast_layer(0)
    for l in range(depth):
        P.epoch = l
        cb = l * NCOLV
        if l + 1 < depth:
            cast_layer(l + 1)
        dma("sp", lnrep[:], lnrep_d[l], [], [r_ln])
        dma("sp", bsT[:], bsT_d[l], [], [r_bsT])
        wst, r_wst = sf()
        dma("sp", wst[:, 0:512], wsT_d[l].rearrange("p g t -> p (g t)"), [], [r_wst])
        trl, r_trl = sf()
        dma("sp", trl[:, 0:128], tril_d, [], [r_trl])
        for g in range(4):
            tt(wsT[:, g, :], wst[:, g * 128:(g + 1) * 128], trl[:, 0:128], ALU.mult, [r_wst, r_trl], [r_wsT])
        pbd, r_pbd = sf()
        dma("sp", pbd[:, 0:256], poolbd_d[l].rearrange("p c q -> p (c q)"), [], [r_pbd])
        cp(poolbd[:].rearrange("p c q -> p (c q)"), pbd[:, 0:256], [r_pbd], [r_poolbd])
        for j in range(NFF):
            memset(carry[:, j, :], 0.0, [r_carry[j]])
        for s4 in range(4):
            memset(vA[:, s4, :], 0.0, [r_vAB[s4]])
            memset(vB[:, s4, :], 0.0, [r_vAB[s4]])

        for it in range(NT):
            t0 = it * T
            for kc in range(8):
                pass
            for kc in range(8):
                if l == 0:
                    src = xT_d[kc * 128:(kc + 1) * 128, t0:t0 + T]
                    P.op("sp", lambda e, src=src, kc=kc: e.dma_start(out=xT[:, kc, :], in_=src), [], [r_xT[kc]], dma=True)
                else:
                    P.op("sp", lambda e, it=it, kc=kc: e.dma_start(out=xT[:, kc, :], in_=xs_d[it, :, kc, :]),
                         [r_xs[kc]], [r_xT[kc]], dma=True)
            rmsnorm_to_hT(l, cb + 0)

            winv = wview(win_bf[l])
            wA, r_wA = load_blk(l, winv[:, :, 0:512], 8, 512)
            uT = [pb[:, 18:20, :].rearrange("p a t -> p (a t)").bitcast(F32),
                  pb[:, 20:22, :].rearrange("p a t -> p (a t)").bitcast(F32)]
            r_uT = [[r_pb[18], r_pb[19]], [r_pb[20], r_pb[21]]]
            for c in range(2):
                for kc in range(8):
                    mm(bank[c][:], wA[:, kc, c * 128:(c + 1) * 128], hT[:, kc, :], kc == 0, kc == 7,
                       [r_wA, r_hT[kc]], [r_bank[c]])
                act(uT[c], bank[c][:], AF.Gelu, [r_bank[c]], r_uT[c])
            vgs = []
            for s4 in range(4):
                bk = 2 + (s4 % 2)
                for kc in range(8):
                    mm(bank[bk][:, 0:256], hT[:, kc, s4 * 128:(s4 + 1) * 128], wA[:, kc, 256:512], kc == 0, kc == 7,
                       [r_wA, r_hT[kc]], [r_bank[bk]])
                vg, r_vg = sf()
                vgs.append((vg, r_vg))
                act(vg[:, 0:256], bank[bk][:, 0:256], AF.Gelu, [r_bank[bk]], [r_vg])
                P.op("dve", lambda e, vg=vg: e.bn_stats(out=small[:, 0:6], in_=vg[:, 0:256]), [r_vg], [r_small])
                P.op("dve", lambda e, s4=s4: e.bn_aggr(out=small[:, 8 + 2 * s4:10 + 2 * s4], in_=small[:, 0:6]),
                     [r_small], [r_small])
            var4 = small[:, 8:16].rearrange("p (s two) -> p s two", two=2)[:, :, 1]
            act(small2[:, 0:4], var4, AF.Ln, [r_small, r_const], [r_small], bias=epsb[:, 1:2])
            act(small2[:, 0:4], small2[:, 0:4], AF.Exp, [r_small], [r_small], scale=-0.5)

            def gview(ap2d, b):
                return ap2d.rearrange("p (a b d) -> p a b d", a=2, b=2)[:, :, b, :]
            for s4 in range(4):
                vg, r_vg = vgs[s4]
                ts(vg[:, 0:256], vg[:, 0:256], small[:, 8 + 2 * s4:9 + 2 * s4], small2[:, s4:s4 + 1], ALU.subtract, ALU.mult,
                   [r_vg, r_small], [r_vg])
                tt(vg[:, 0:256], vg[:, 0:256], lnrep[:, 0, :], ALU.mult, [r_vg, r_ln], [r_vg])
                tt(gview(vA[:, s4, :], 0), gview(vg[:, 0:256], 0), gview(lnrep[:, 1, :], 0), ALU.add,
                   [r_vg, r_ln], [r_vAB[s4]])
                tt(gview(vB[:, s4, :], 1), gview(vg[:, 0:256], 1), gview(lnrep[:, 1, :], 1), ALU.add,
                   [r_vg, r_ln], [r_vAB[s4]])
            yaT = [pb[:, 8, :], pb[:, 9, :]]
            for c in range(2):
                bk = 4 + c
                for s4 in range(4):
                    mm(bank[bk][:, s4 * 128:(s4 + 1) * 128], vA[:, s4, c * 128:(c + 1) * 128], wsT[:, 2 * c, :],
                       True, False, [r_vAB[s4], r_wsT], [r_bank[bk]])
                    mm(bank[bk][:, s4 * 128:(s4 + 1) * 128], vB[:, s4, c * 128:(c + 1) * 128], wsT[:, 2 * c + 1, :],
                       False, True, [r_vAB[s4], r_wsT], [r_bank[bk]])
                tm, r_tm = sf()
                tt(tm[:, 0:T].rearrange("p (s t) -> p s t", s=4), bank[bk][:].rearrange("p (s t) -> p s t", s=4),
                   bsT[:, c, :].unsqueeze(1).broadcast_to([128, 4, 128]), ALU.add, [r_bank[bk], r_bsT], [r_tm])
                tt(yaT[c], tm[:, 0:T], uT[c], ALU.mult, [r_tm] + r_uT[c], [r_pb[8 + c]])
            if l == 0 and it == 0:
                dump("yaT", pb[:, 8:10, :], [r_pb[8], r_pb[9]], [128, 2, T], BF16)

            wB, r_wB = load_blk(l, winv[:, :, 512:768], 8, 256)
            pT = [pb[:, 16, :], pb[:, 17, :]]
            ybT = [pb[:, 10, :], pb[:, 11, :]]
            for c in range(2):
                bk = 6 + c
                for kc in range(8):
                    mm(bank[bk][:], wB[:, kc, c * 128:(c + 1) * 128], hT[:, kc, :], kc == 0, kc == 7,
                       [r_wB, r_hT[kc]], [r_bank[bk]])
                if it == 0:
                    memset(XB[:, c, 0:16], 0.0, [r_XB[c]])
                else:
                    cp(XB[:, c, 0:16], XB[:, c, T:T + 16], [r_XB[c]], [r_XB[c]])
                P.op("act", lambda e, c=c, bk=bk: e.activation(out=XB[:, c, 16:16 + T], in_=bank[bk][:], func=AF.Copy),
                     [r_bank[bk]], [r_XB[c]])
                A_, r_A = sf()
                B_, r_B = sf()
                W_ = T + 16
                x_ = XB[:, c, :]
                tt(A_[:, 1:W_], x_[:, 1:W_], x_[:, 0:W_ - 1], ALU.add, [r_XB[c]], [r_A])
                if c == 0:
                    tt(B_[64:128, 3:W_], A_[64:128, 3:W_], A_[64:128, 1:W_ - 2], ALU.add, [r_A], [r_B])
                    lo, hi = A_, B_
                else:
                    tt(B_[:, 3:W_], A_[:, 3:W_], A_[:, 1:W_ - 2], ALU.add, [r_A], [r_B])
                    tt(A_[:, 7:W_], B_[:, 7:W_], B_[:, 3:W_ - 4], ALU.add, [r_B, r_A], [r_A])
                    tt(B_[64:128, 15:W_], A_[64:128, 15:W_], A_[64:128, 7:W_ - 8], ALU.add, [r_A, r_B], [r_B])
                    lo, hi = A_, B_
                for half, wt_ in ((0, lo), (1, hi)):
                    ps = slice(half * 64, half * 64 + 64)
                    stt(pT[c][ps, :], wt_[ps, 16:W_], constk[ps, c:c + 1], x_[ps, 16:W_], ALU.mult, ALU.subtract,
                        [r_A, r_B, r_XB[c], r_const], [r_pb[16 + c]])
                    if it == 0:
                        f_, r_f = sf()
                        tt(f_[ps, 0:16], wt_[ps, 16:32], constk[ps, 2 + c * 16:2 + c * 16 + 16], ALU.mult,
                           [r_A, r_B, r_const], [r_f])
                        tt(pT[c][ps, 0:16], f_[ps, 0:16], x_[ps, 16:32], ALU.subtract, [r_f, r_XB[c]], [r_pb[16 + c]])
                bk2 = 2 + c
                mm(bank[bk2][:], poolbd[:, c, :], pT[c], True, True, [r_poolbd, r_pb[16 + c]], [r_bank[bk2]])
                ts(ybT[c], bank[bk2][:], colv[:, cb + 128 + c:cb + 129 + c], None, ALU.mult, None,
                   [r_bank[bk2], r_colv], [r_pb[10 + c]])
            if l == 0 and it == 0:
                dump("ybT", pb[:, 10:12, :], [r_pb[10], r_pb[11]], [128, 2, T], BF16)

            wQ, r_wQ = load_blk(l, winv[:, :, 768:1280], 8, 512)
            for h in range(4):
                bk = h % 4
                for kc in range(8):
                    mm(bank[bk][:], wQ[:, kc, h * 128:(h + 1) * 128], hT[:, kc, :], kc == 0, kc == 7,
                       [r_wQ, r_hT[kc]], [r_bank[bk]])
                cp(QT[:, h, :], bank[bk][:], [r_bank[bk]], [r_QT[h]])
            wK, r_wK = load_blk(l, winv[:, :, 1280:1792], 8, 512)
            for h in range(4):
                bk = 4 + h % 4
                for kc in range(8):
                    mm(bank[bk][:], wK[:, kc, h * 128:(h + 1) * 128], hT[:, kc, :], kc == 0, kc == 7,
                       [r_wK, r_hT[kc]], [r_bank[bk]])
                P.op("act", lambda e, h=h, bk=bk, t0=t0: e.activation(out=KT[:, h, t0:t0 + T], in_=bank[bk][:], func=AF.Copy),
                     [r_bank[bk]], [r_KT[h][it]])
            wV, r_wV = load_blk(l, winv[:, :, 1792:2304], 8, 512)
            for s4 in range(4):
                bk = s4 % 4
                kb = it * 4 + s4
                for kc in range(8):
                    mm(bank[bk][:], hT[:, kc, s4 * 128:(s4 + 1) * 128], wV[:, kc, :], kc == 0, kc == 7,
                       [r_wV, r_hT[kc]], [r_bank[bk]])
                cp(V[:, kb, :], bank[bk][:], [r_bank[bk]], [r_V[kb]])

            ycT = [pb[:, 12 + h, :] for h in range(4)]
            nkb = 4 * it + 4
            for h in range(4):
                SB = [0, 1, 7]
                OB = [2, 3]
                RB = [4, 5]
                NB_ = 6
                pend = []
                sidx = 0
                pidx = 0

                def flush(pend):
                    for (c, kb, q0, PTc, r_PT) in pend:
                        mm(bank[OB[c]][:, q0:T], V[:, kb, h * 128:(h + 1) * 128], PTc[:, q0:T], kb == 0, kb == nkb - 1,
                           [r_V[kb], r_PT], [r_bank[OB[c]]])
                        mm(bank[RB[c]][:, q0:T], ones_bf[:], PTc[:, q0:T], kb == 0, kb == nkb - 1,
                           [r_ones, r_PT], [r_bank[RB[c]]])

                for kb in range(nkb):
                    j = kb - 4 * it
                    q0 = 128 * max(0, j)
                    kit = kb // 4
                    PT = ptb[pidx % 3]
                    r_PT = r_ptb[pidx % 3]
                    pidx += 1
                    sbks = []
                    for c in range(2):
                        sbk = SB[sidx % 3]
                        sidx += 1
                        sbks.append(sbk)
                        cs = slice(c * 64, c * 64 + 64)
                        mm(bank[sbk][:, q0:T], KT[cs, h, kb * 128:(kb + 1) * 128], QT[cs, h, q0:T], True, True,
                           [r_KT[h][kit], r_QT[h]], [r_bank[sbk]])
                    for c in range(2):
                        sbk = sbks[c]
                        if j < -1:
                            act(PT[:, c, q0:T], bank[sbk][:, q0:T], AF.Exp, [r_bank[sbk], r_const], [r_PT],
                                bias=b31[:, h:h + 1], scale=scale)
                        else:
                            tmp, r_tmp = sf()
                            off = 128 if j == -1 else 0
                            stt(tmp[:, q0:T], bank[sbk][:, q0:T], scale, bcat[:, h, off:off + T - q0], ALU.mult, ALU.add,
                                [r_bank[sbk], r_dbias], [r_tmp])
                            act(PT[:, c, q0:T], tmp[:, q0:T], AF.Exp, [r_tmp], [r_PT])
                    newp = [(c, kb, q0, PT[:, c, :], r_PT) for c in range(2)]
                    flush(pend)
                    pend = newp
                flush(pend)
                i0, r_i0 = sf()
                i1, r_i1 = sf()
                act(i0[:, 0:T], bank[RB[0]][:], AF.Ln, [r_bank[RB[0]]], [r_i0])
                act(i1[:, 0:T], bank[RB[1]][:], AF.Ln, [r_bank[RB[1]]], [r_i1])
                act(i0[:, 0:T], i0[:, 0:T], AF.Exp, [r_i0], [r_i0], scale=-1.0)
                act(i1[:, 0:T], i1[:, 0:T], AF.Exp, [r_i1], [r_i1], scale=-1.0)
                tt(i0[:, 0:T], bank[OB[0]][:], i0[:, 0:T], ALU.mult, [r_bank[OB[0]], r_i0], [r_i0])
                tt(i1[:, 0:T], bank[OB[1]][:], i1[:, 0:T], ALU.mult, [r_bank[OB[1]], r_i1], [r_i1])
                stt(i0[:, 0:T], i1[:, 0:T], lamt[:, l, 0:1], i0[:, 0:T], ALU.mult, ALU.add, [r_i0, r_i1, r_lam], [r_i0])
                sq, r_sq = sbf()
                tt(sq[:], i0[:, 0:T], i0[:, 0:T], ALU.mult, [r_i0], [r_sq], eng="pool" if POOL_SQ else "dve")
                mm(bank[NB_][:], ones_bf[:], sq[:], True, True, [r_ones, r_sq], [r_bank[NB_]])
                act(i1[:, 0:T], bank[NB_][:], AF.Ln, [r_bank[NB_], r_const], [r_i1], bias=epsb[:, 1:2], scale=1.0 / 128)
                act(i1[:, 0:T], i1[:, 0:T], AF.Exp, [r_i1], [r_i1], scale=-0.5)
                stt(ycT[h], i0[:, 0:T], gsub[:, l:l + 1], i1[:, 0:T], ALU.mult, ALU.mult, [r_i0, r_i1, r_lam],
                    [r_pb[12 + h]])
            if l == 0 and it == 0:
                dump("ycT", pb[:, 12:16, :], [r_pb[12 + h] for h in range(4)], [128, 4, T], BF16)

            wgv = wview(wgate_bf[l])
            prv = wview(proj_bf[l])
            for half in range(2):
                wg = []
                for br in range(3):
                    c0 = br * 1024 + half * 512
                    wg.append(load_blk(l, wgv[:, :, c0:c0 + 512], 8, 512))
                wp, r_wp = load_blk(l, prv[:, :, half * 512:half * 512 + 512], 8, 512)
                for ml in range(4):
                    m = half * 4 + ml
                    cs = slice(ml * 128, ml * 128 + 128)
                    sg = []
                    for br in range(3):
                        for kc in range(8):
                            mm(bank[br][:], wg[br][0][:, kc, cs], hT[:, kc, :], kc == 0, kc == 7,
                               [wg[br][1], r_hT[kc]], [r_bank[br]])
                        s_, r_s = sf()
                        act(s_[:, 0:T], bank[br][:], AF.Sigmoid, [r_bank[br], r_colv], [r_s],
                            bias=colv[:, cb + 16 + br * 8 + m:cb + 17 + br * 8 + m])
                        sg.append((s_, r_s))
                    srcs = [([pb[:, 8, :], pb[:, 9, :]], [r_pb[8], r_pb[9]], 0),
                            ([pb[:, 10, :], pb[:, 11, :]], [r_pb[10], r_pb[11]], 2),
                            ([pb[:, 12 + h, :] for h in range(4)], [r_pb[12 + h] for h in range(4)], 4)]
                    for br in range(3):
                        ys, rys, k0 = srcs[br]
                        for i_, (y_, ry_) in enumerate(zip(ys, rys)):
                            mm(bank[3 + br][:], wp[:, k0 + i_, cs], y_, i_ == 0, i_ == len(ys) - 1,
                               [r_wp, ry_], [r_bank[3 + br]])
                        s_, r_s = sg[br]
                        tt(s_[:, 0:T], bank[3 + br][:], s_[:, 0:T], ALU.mult, [r_bank[3 + br], r_s], [r_s])
                    tt(sg[0][0][:, 0:T], sg[0][0][:, 0:T], sg[1][0][:, 0:T], ALU.add, [sg[0][1], sg[1][1]], [sg[0][1]])
                    tt(pb[:, m, :], sg[0][0][:, 0:T], sg[2][0][:, 0:T], ALU.add, [sg[0][1], sg[2][1]], [r_pb[m]])
            if l == 0 and it == 0:
                dump("mergedT", pb[:, 0:8, :], [r_pb[m] for m in range(8)], [128, 8, T], BF16)

            wov = wview(wout_bf[l])
            for half in range(2):
                wo, r_wo = load_blk(l, wov[:, :, half * 512:half * 512 + 512], 8, 512)
                for ml in range(4):
                    m = half * 4 + ml
                    bk = 6 + (ml % 2)
                    for kc in range(8):
                        mm(bank[bk][:], wo[:, kc, ml * 128:(ml + 1) * 128], pb[:, kc, :], kc == 0, kc == 7,
                           [r_wo, r_pb[kc]], [r_bank[bk]])
                    tt(xT[:, m, :], xT[:, m, :], bank[bk][:], ALU.add, [r_xT[m], r_bank[bk]], [r_xT[m]])
            if l == 0 and it == 0:
                dump("x1T", xT[:], r_xT, [128, 8, T], F32)

            rmsnorm_to_hT(l, cb + 8)
            wuv = wview(wup_bf[l])
            cw = cb + 40

            def ffn_stage_a(j, wG, r_wG, wU, r_wU, jj):
                gb = j % 3
                ub = 3 + j % 3
                cs = slice(jj * 128, jj * 128 + 128)
                for kc in range(8):
                    mm(bank[gb][:], wG[:, kc, cs], hT[:, kc, :], kc == 0, kc == 7, [r_wG, r_hT[kc]], [r_bank[gb]])
                for kc in range(8):
                    mm(bank[ub][:], wU[:, kc, cs], hT[:, kc, :], kc == 0, kc == 7, [r_wU, r_hT[kc]], [r_bank[ub]])
                G_, r_G = sf()
                cp(G_[:, 0:2], carry[:, j, :], [r_carry[j]], [r_G])
                P.op("act", lambda e, G_=G_, gb=gb: e.activation(out=G_[:, 2:2 + T], in_=bank[gb][:], func=AF.Copy),
                     [r_bank[gb]], [r_G])
                cp(carry[:, j, :], G_[:, T:T + 2], [r_G], [r_carry[j]])
                a_, r_a = sf()
                ts(a_[:, 0:T], G_[:, 0:T], colv[:, cw + j:cw + j + 1], colv[:, cb + 106 + j:cb + 107 + j],
                   ALU.mult, ALU.add, [r_G, r_colv], [r_a])
                stt(a_[:, 0:T], G_[:, 1:T + 1], colv[:, cw + 22 + j:cw + 23 + j], a_[:, 0:T], ALU.mult, ALU.add,
                    [r_G, r_a, r_colv], [r_a])
                stt(a_[:, 0:T], G_[:, 2:T + 2], colv[:, cw + 44 + j:cw + 45 + j], a_[:, 0:T], ALU.mult, ALU.add,
                    [r_G, r_a, r_colv], [r_a])
                return (j, a_, r_a, ub)

            def ffn_stage_b(stt_):
                j, a_, r_a, ub = stt_
                act(a_[:, 0:T], a_[:, 0:T], AF.Gelu, [r_a], [r_a])
                tt(pb[:, j, :], a_[:, 0:T], bank[ub][:], ALU.mult, [r_a, r_bank[ub]], [r_pb[j]])

            prev = None
            for jb in range(6):
                mw = 512 if jb < 5 else 256
                wG, r_wG = load_blk(l, wuv[:, :, jb * 512:jb * 512 + mw], 8, mw)
                wU, r_wU = load_blk(l, wuv[:, :, DFF + jb * 512:DFF + jb * 512 + mw], 8, mw)
                for jj in range(mw // 128):
                    j = jb * 4 + jj
                    cur = ffn_stage_a(j, wG, r_wG, wU, r_wU, jj)
                    if prev is not None:
                        ffn_stage_b(prev)
                    prev = cur
            ffn_stage_b(prev)
            wdv = wview(wdn_bf[l])
            for mb in range(2):
                for kt in range(3):
                    nk = 8 if kt < 2 else 6
                    wd, r_wd = load_blk(l, wdv[:, kt * 8:kt * 8 + nk, mb * 512:mb * 512 + 512], nk, 512)
                    for ml in range(4):
                        bk = 4 + ml
                        for kk in range(nk):
                            kc = kt * 8 + kk
                            mm(bank[bk][:], wd[:, kk, ml * 128:(ml + 1) * 128], pb[:, kc, :], kc == 0, kc == NFF - 1,
                               [r_wd, r_pb[kc]], [r_bank[bk]])
                for ml in range(4):
                    m = mb * 4 + ml
                    tt(xT[:, m, :], xT[:, m, :], bank[4 + ml][:], ALU.add, [r_xT[m], r_bank[4 + ml]], [r_xT[m]])
                    if l < depth - 1:
                        P.op("act", lambda e, it=it, m=m: e.dma_start(out=xs_d[it, :, m, :], in_=xT[:, m, :]),
                             [r_xT[m]], [r_xs[m]], dma=True)

            if l == depth - 1:
                gcol = depth * NCOLV
                nb = 7
                for kc in range(8):
                    sq, r_sq = sbf()
                    act(sq[:], xT[:, kc, :], AF.Square, [r_xT[kc]], [r_sq])
                    mm(bank[nb][:], ones_bf[:], sq[:], kc == 0, kc == 7, [r_ones, r_sq], [r_bank[nb]])
                sd, r_sd = sf()
                act(sd[:, 0:T], bank[nb][:], AF.Ln, [r_bank[nb], r_const], [r_sd], bias=epsb[:, 0:1], scale=1.0 / D)
                act(sd[:, 0:T], sd[:, 0:T], AF.Exp, [r_sd], [r_sd], scale=-0.5)
                for kc in range(8):
                    stt(xT[:, kc, :], xT[:, kc, :], colv[:, gcol + kc:gcol + kc + 1], sd[:, 0:T], ALU.mult, ALU.mult,
                        [r_xT[kc], r_colv, r_sd], [r_xT[kc]])
                dst = outT_d.rearrange("(kc p) t -> p kc t", p=128)[:, :, t0:t0 + T]
                P.op("sp", lambda e, dst=dst: e.dma_start(out=dst, in_=xT[:]), r_xT, [r_out], dma=True)

    P.epoch = depth
    fin = [r_out] + list(dbg_out.values())
    P.op("sp", lambda e: None, fin, [])
    P.emit(st)
    st.close()
    return nc, P


def _t5_bucket_np(rel):
    n = np.maximum(rel, 0)
    nf = np.maximum(n, 1).astype(np.float32)
    large = 16 + (np.log(nf / np.float32(16)) / np.float32(math.log(128 / 16)) * np.float32(16)).astype(np.int32)
    large = np.minimum(large, 31)
    return np.where(n < 16, n, large)


def _bucket_tables():
    k = np.arange(128)[:, None]
    q = np.arange(128)[None, :]
    try:
        import jax
        import jax.numpy as jnp

        def bk(rel):
            nn = jnp.maximum(rel, 0)
            nf = jnp.maximum(nn, 1).astype(jnp.float32)
            large = 16 + (jnp.log(nf / 16) / math.log(128 / 16) * 16).astype(jnp.int32)
            large = jnp.minimum(large, 31)
            return np.asarray(jnp.where(nn < 16, nn, large))
        with jax.default_device(jax.devices("cpu")[0]):
            b0 = bk(jnp.asarray(q - k))
            b1 = bk(jnp.asarray(128 + q - k))
        return b0, b1
    except Exception:
        return _t5_bucket_np(q - k), _t5_bucket_np(128 + q - k)


def prep_inputs(inp, depth, b, S):
    f = np.float32
    m = {}
    m["xT"] = np.ascontiguousarray(inp["x"][b, :S].T)
    for k in ("w_in", "w_gate", "proj_a", "proj_b", "proj_c", "w_out", "w_up", "w_down"):
        m[k] = np.ascontiguousarray(inp[k][:depth])
    NCOL = depth * NCOLV + 8
    colv = np.zeros((128, NCOL), f)
    for l in range(depth):
        cb = l * NCOLV
        colv[:, cb + 0:cb + 8] = inp["attn_norm_g"][l].reshape(8, 128).T
        colv[:, cb + 8:cb + 16] = inp["ffn_norm_g"][l].reshape(8, 128).T
        colv[:, cb + 16:cb + 40] = inp["b_gate"][l].reshape(24, 128).T
        for tap in range(3):
            colv[:, cb + 40 + tap * 22:cb + 62 + tap * 22] = inp["conv_w"][l, tap].reshape(22, 128).T
        colv[:, cb + 106:cb + 128] = inp["conv_b"][l].reshape(22, 128).T
        colv[:, cb + 128:cb + 130] = inp["pool_scale"][l].reshape(2, 128).T
        colv[:, cb + 130] = inp["diff_subln_g"][l]
    colv[:, depth * NCOLV:depth * NCOLV + 8] = inp["final_norm_g"].reshape(8, 128).T
    m["colv"] = colv
    m["lamrep"] = np.ascontiguousarray(np.broadcast_to(inp["diff_lam"][:depth].reshape(1, depth * 256), (128, depth * 256)))
    m["b31rep"] = np.ascontiguousarray(np.broadcast_to(inp["rel_bias"][31][None, :], (128, 4)))
    b0, b1 = _bucket_tables()
    rb = inp["rel_bias"]
    dbias = np.zeros((128, 4, 2, 128), f)
    for h in range(4):
        dbias[:, h, 0, :] = rb[b0, h]
        dbias[:, h, 1, :] = rb[b1, h]
    m["dbias"] = dbias
    k = np.arange(128)[:, None]
    q = np.arange(128)[None, :]
    m["maskneg"] = np.where(q >= k, 0.0, -30000.0).astype(f)
    m["tril01"] = np.where(q >= k, 1.0, 0.0).astype(f)
    ln = np.zeros((depth, 128, 2, 256), f)
    ln[:, :, 0, :] = inp["sgu_ln_g"][:depth, None, :]
    ln[:, :, 1, :] = inp["sgu_ln_b"][:depth, None, :]
    m["lnrep"] = ln
    bs = np.zeros((depth, 128, 2, 128), f)
    for c in range(2):
        for hf in range(2):
            bs[:, hf * 64:(hf + 1) * 64, c, :] = inp["sgu_b"][:depth, 2 * c + hf][:, None, :]
    m["bsT"] = bs
    m["wsT"] = np.ascontiguousarray(np.transpose(inp["sgu_w"][:depth], (0, 3, 1, 2)))
    pbd = np.zeros((depth, 128, 2, 128), f)
    for c in range(2):
        for hf in range(2):
            pbd[:, hf * 64:(hf + 1) * 64, c, hf * 64:(hf + 1) * 64] = inp["pool_w"][:depth, 2 * c + hf]
    m["poolbd"] = pbd
    ck = np.zeros((128, 34), f)
    wins = [[2, 4], [8, 16]]
    for c in range(2):
        for hf in range(2):
            w = wins[c][hf]
            ck[hf * 64:(hf + 1) * 64, c] = 1.0 / w
            ck[hf * 64:(hf + 1) * 64, 2 + c * 16:2 + c * 16 + 16] = 1.0 / np.minimum(np.arange(16) + 1, w)
    m["constk"] = ck
    return m


_CACHE = {}


def kernel(**inputs):
    inp = {k: np.asarray(v) for k, v in inputs.items()}
    B, S, _ = inp["x"].shape
    depth = inp["w_in"].shape[0]
    key = (S, depth)
    if key not in _CACHE:
        _CACHE[key] = build(S=S, depth=depth)[0]
    nc = _CACHE[key]
    in_maps = [prep_inputs(inp, depth, b, S) for b in range(B)]
    res = run_bass_kernel_spmd(nc, in_maps, core_ids=list(range(B)))
    out = np.stack([np.asarray(r["outT"]).T for r in res.results], axis=0)
    return np.ascontiguousarray(out.astype(np.float32))
```

```python
import math
from contextlib import ExitStack

import numpy as np
import concourse.bass as bass
import concourse.mybir as mybir
from concourse.bass_utils import run_bass_kernel_spmd

F32 = mybir.dt.float32
BF16 = mybir.dt.bfloat16
AF = mybir.ActivationFunctionType
ALU = mybir.AluOpType

D = 1024
DFF = 2816
NFF = DFF // 128
T = 512
NCOLV = 131
PAIR_EXP = False
POOL_SQ = True


class Res:
    __slots__ = ("name", "lw", "rd", "chan")

    def __init__(self, name):
        self.name = name
        self.lw = None
        self.rd = []
        self.chan = None


class Ins:
    __slots__ = ("eng", "fn", "deps", "sig", "val", "is_dma", "chan", "epoch")


class Prog:
    ENGS = ("pe", "act", "dve", "pool", "sp")

    def __init__(self, nc, n_epochs=1, same_engine_sync=True):
        self.nc = nc
        self.q = {e: [] for e in self.ENGS}
        self.n_epochs = n_epochs
        self.epoch = 0
        self.same_engine_sync = same_engine_sync
        self.chans = []

    def res(self, name):
        return Res(name)

    def op(self, eng, fn, reads=(), writes=(), dma=False):
        ins = Ins()
        ins.eng = eng
        ins.fn = fn
        ins.is_dma = dma
        ins.epoch = self.epoch
        ins.sig = False
        ins.val = None
        ins.chan = None
        deps = set()
        for r in reads:
            if r.lw is not None:
                deps.add(r.lw)
        for w in writes:
            if w.lw is not None:
                deps.add(w.lw)
            last = {}
            for rdr in w.rd:
                if rdr.is_dma:
                    deps.add(rdr)
                else:
                    last[rdr.eng] = rdr
            deps.update(last.values())
        deps.discard(ins)
        for r in reads:
            r.rd.append(ins)
        for w in writes:
            w.lw = ins
            w.rd = []
        if dma:
            assert len(writes) == 1
            w = writes[0]
            if w.chan is None:
                w.chan = {"sem": None, "count": 0, "name": w.name}
                self.chans.append(w.chan)
            ins.chan = w.chan
        pruned = set()
        for d in deps:
            if d.is_dma:
                pruned.add(d)
            elif d.eng == eng:
                if eng in ("pe", "sp"):
                    continue
                if dma or self.same_engine_sync:
                    pruned.add(d)
            else:
                pruned.add(d)
        ins.deps = pruned
        for d in pruned:
            d.sig = True
        self.q[eng].append(ins)
        return ins

    def emit(self, stack):
        nc = self.nc
        esem = {}
        for e in self.ENGS:
            for ep in range(self.n_epochs):
                esem[(e, ep)] = stack.enter_context(nc.semaphore(f"s_{e}_{ep}"))
        for i, ch in enumerate(self.chans):
            ch["sem"] = stack.enter_context(nc.semaphore(f"d_{i}"))
        cnt = {}
        for e in self.ENGS:
            for ins in self.q[e]:
                if ins.is_dma:
                    ins.chan["count"] += 16
                    ins.val = ins.chan["count"]
                elif ins.sig:
                    k = (e, ins.epoch)
                    cnt[k] = cnt.get(k, 0) + 1
                    ins.val = cnt[k]
        self.stats = {e: len(self.q[e]) for e in self.ENGS}
        self.stats["cnt"] = {f"{k[0]}{k[1]}": v for k, v in cnt.items()}
        block = stack.enter_context(nc.Block())

        def run(e, engobj):
            waited = {}
            nw = 0
            for ins in self.q[e]:
                need = {}
                for d in ins.deps:
                    if d.is_dma:
                        key = ("dma", id(d.chan))
                        sem = d.chan["sem"]
                    else:
                        key = (d.eng, d.epoch)
                        sem = esem[key]
                    v = d.val
                    if waited.get(key, 0) >= v:
                        continue
                    if key not in need or need[key][1] < v:
                        need[key] = (sem, v)
                for key, (sem, v) in need.items():
                    engobj.wait_ge(sem, v)
                    waited[key] = v
                    nw += 1
                bi = ins.fn(engobj)
                if bi is None:
                    continue
                if ins.is_dma:
                    bi.then_inc(ins.chan["sem"], 16)
                elif ins.sig:
                    bi.then_inc(esem[(e, ins.epoch)], 1)
            self.stats["waits_" + e] = nw

        block.sync(lambda eng: run("sp", eng))
        block.tensor(lambda eng: run("pe", eng))
        block.scalar(lambda eng: run("act", eng))
        block.vector(lambda eng: run("dve", eng))
        block.gpsimd(lambda eng: run("pool", eng))


def build(S=4096, depth=4, dbg=False, layer0=0):
    NT = S // T
    nc = bass.Bass("TRN2", target_bir_lowering=False)
    P = Prog(nc, n_epochs=depth + 1)
    st = ExitStack()

    def din(name, shape, dt=F32):
        return nc.dram_tensor(name, list(shape), dt, kind="ExternalInput").ap()

    xT_d = din("xT", [D, S])
    w_in_d = din("w_in", [depth, D, 2304])
    w_gate_d = din("w_gate", [depth, D, 3072])
    proj_a_d = din("proj_a", [depth, 256, D])
    proj_b_d = din("proj_b", [depth, 256, D])
    proj_c_d = din("proj_c", [depth, 512, D])
    w_out_d = din("w_out", [depth, D, D])
    w_up_d = din("w_up", [depth, D, 2 * DFF])
    w_down_d = din("w_down", [depth, DFF, D])
    NCOL = depth * NCOLV + 8
    colv_d = din("colv", [128, NCOL])
    lamrep_d = din("lamrep", [128, depth * 256])
    b31_d = din("b31rep", [128, 4])
    dbias_d = din("dbias", [128, 4, 2, 128])
    maskneg_d = din("maskneg", [128, 128])
    tril_d = din("tril01", [128, 128])
    lnrep_d = din("lnrep", [depth, 128, 2, 256])
    bsT_d = din("bsT", [depth, 128, 2, 128])
    wsT_d = din("wsT", [depth, 128, 4, 128])
    poolbd_d = din("poolbd", [depth, 128, 2, 128])
    constk_d = din("constk", [128, 34])
    outT_d = nc.dram_tensor("outT", [D, S], F32, kind="ExternalOutput").ap()

    win_bf = nc.dram_tensor("win_bf", [depth, D, 2304], BF16).ap()
    wgate_bf = nc.dram_tensor("wgate_bf", [depth, D, 3072], BF16).ap()
    proj_bf = nc.dram_tensor("proj_bf", [depth, D, D], BF16).ap()
    wout_bf = nc.dram_tensor("wout_bf", [depth, D, D], BF16).ap()
    wup_bf = nc.dram_tensor("wup_bf", [depth, D, 2 * DFF], BF16).ap()
    wdn_bf = nc.dram_tensor("wdn_bf", [depth, DFF, D], BF16).ap()
    xs_d = nc.dram_tensor("xs", [NT, 128, 8, T], F32).ap()
    r_xs = [P.res(f"xs{k}") for k in range(8)]
    r_out = P.res("out")

    dbg_out = {}

    def sb(name, shape, dt):
        return nc.alloc_sbuf_tensor("s_" + name, list(shape), dt)

    xT = sb("xTt", [128, 8, T], F32)
    r_xT = [P.res(f"xT{k}") for k in range(8)]
    hT = sb("hT", [128, 8, T], BF16)
    r_hT = [P.res(f"hT{k}") for k in range(8)]
    KT = sb("KT", [128, 4, S], BF16)
    r_KT = [[P.res(f"KT{h}_{i}") for i in range(NT)] for h in range(4)]
    V = sb("V", [128, S // 128, 512], BF16)
    r_V = [P.res(f"V{kb}") for kb in range(S // 128)]
    QT = sb("QT", [128, 4, T], BF16)
    r_QT = [P.res(f"QT{h}") for h in range(4)]
    vA = sb("vA", [128, 4, 256], BF16)
    vB = sb("vB", [128, 4, 256], BF16)
    r_vAB = [P.res(f"vAB{s}") for s in range(4)]
    XB = sb("XB", [128, 2, 16 + T], F32)
    r_XB = [P.res(f"XB{c}") for c in range(2)]
    pb = sb("pb", [128, NFF, T], BF16)
    r_pb = [P.res(f"pb{j}") for j in range(NFF)]
    NSLOT = 5
    wslot = [sb(f"wslot{i}", [128, 8, 512], BF16) for i in range(NSLOT)]
    r_wslot = [P.res(f"wslot{i}") for i in range(NSLOT)]
    NSF = 8
    sf_t = [sb(f"sf{i}", [128, T + 16], F32) for i in range(NSF)]
    r_sf = [P.res(f"sf{i}") for i in range(NSF)]
    NSB = 4
    sbf_t = [sb(f"sbf{i}", [128, T], BF16) for i in range(NSB)]
    r_sbf = [P.res(f"sbf{i}") for i in range(NSB)]
    colv = sb("colv", [128, NCOL], F32)
    r_colv = P.res("colv")
    b31 = sb("b31", [128, 4], F32)
    bcat = sb("bcat", [128, 4, 640], F32)
    r_dbias = P.res("bcat")
    constk = sb("constk", [128, 34], F32)
    r_const = P.res("const")
    ones_bf = sb("ones", [128, 128], BF16)
    r_ones = P.res("ones")
    epsb = sb("epsb", [128, 2], F32)
    lamt = sb("lamt", [128, depth, 4], F32)
    r_lam = P.res("lam")
    gsub = sb("gsub", [128, depth], F32)
    lnrep = sb("lnrep", [128, 2, 256], F32)
    r_ln = P.res("ln")
    bsT = sb("bsT", [128, 2, 128], F32)
    r_bsT = P.res("bsT")
    wsT = sb("wsT", [128, 4, 128], BF16)
    r_wsT = P.res("wsT")
    poolbd = sb("poolbd", [128, 2, 128], BF16)
    r_poolbd = P.res("poolbd")
    carry = sb("carry", [128, NFF, 2], F32)
    r_carry = [P.res(f"carry{j}") for j in range(NFF)]
    small = sb("small", [128, 16], F32)
    r_small = P.res("small")
    ps_all = nc.alloc_psum_tensor("ps_all", [128, 8, 512], F32)
    bank = [ps_all[:, i, :] for i in range(8)]
    ptb = [sb(f"ptb{i}", [128, 2, T], BF16) for i in range(3)]
    r_ptb = [P.res(f"ptb{i}") for i in range(3)]
    small2 = sb("small2", [128, 8], F32)
    r_bank = [P.res(f"bank{i}") for i in range(8)]

    sf_i = [0]

    def sf():
        i = sf_i[0] % (NSF - 2)
        sf_i[0] += 1
        return sf_t[i], r_sf[i]

    sb_i = [0]

    def sbf():
        i = sb_i[0] % NSB
        sb_i[0] += 1
        return sbf_t[i], r_sbf[i]

    ws_i = [0]

    def mm(out, lhsT, rhs, start, stop, reads, writes):
        P.op("pe", lambda e: e.matmul(out, lhsT=lhsT, rhs=rhs, start=start, stop=stop), reads, writes)

    def act(out, in_, func, reads, writes, bias=None, scale=1.0):
        if bias is None:
            P.op("act", lambda e: e.activation(out=out, in_=in_, func=func, scale=scale), reads, writes)
        else:
            P.op("act", lambda e: e.activation(out=out, in_=in_, func=func, bias=bias, scale=scale), reads, writes)

    def tt(out, in0, in1, op, reads, writes, eng="dve"):
        P.op(eng, lambda e: e.tensor_tensor(out=out, in0=in0, in1=in1, op=op), reads, writes)

    def ts(out, in0, s1, s2, op0, op1, reads, writes, eng="dve"):
        if s2 is None:
            P.op(eng, lambda e: e.tensor_scalar(out=out, in0=in0, scalar1=s1, scalar2=None, op0=op0), reads, writes)
        else:
            P.op(eng, lambda e: e.tensor_scalar(out=out, in0=in0, scalar1=s1, scalar2=s2, op0=op0, op1=op1), reads, writes)

    def stt(out, in0, scalar, in1, op0, op1, reads, writes, eng="dve"):
        P.op(eng, lambda e: e.scalar_tensor_tensor(out=out, in0=in0, scalar=scalar, in1=in1, op0=op0, op1=op1), reads, writes)

    def cp(out, in_, reads, writes, eng="dve"):
        P.op(eng, lambda e: e.tensor_copy(out=out, in_=in_), reads, writes)

    def recip(out, in_, reads, writes):
        P.op("dve", lambda e: e.reciprocal(out=out, in_=in_), reads, writes)

    def memset(ap, val, writes, eng="dve"):
        P.op(eng, lambda e: e.memset(ap, val), (), writes)

    def dma(eng, out, in_, reads, writes):
        P.op(eng, lambda e: e.dma_start(out=out, in_=in_), reads, writes, dma=True)

    def dump(name, ap, reads, shape, dt=F32):
        if not dbg:
            return
        d = nc.dram_tensor("dbg_" + name, list(shape), dt, kind="ExternalOutput").ap()
        r = P.res("dbg_" + name)
        dma("sp", d, ap, reads, [r])
        dbg_out[name] = r

    dma("sp", colv[:], colv_d, [], [r_colv])
    dma("sp", b31[:], b31_d, [], [r_const])
    dma("sp", constk[:], constk_d, [], [r_const])
    mt, r_mt = sf()
    dma("sp", mt[:, 0:128], maskneg_d, [], [r_mt])
    for h in range(4):
        dbt, r_dbt = sf()
        dma("sp", dbt[:, 0:256], dbias_d[:, h].rearrange("p a q -> p (a q)"), [], [r_dbt])
        tt(bcat[:, h, 0:128], dbt[:, 0:128], mt[:, 0:128], ALU.add, [r_dbt, r_mt], [r_dbias])
        cp(bcat[:, h, 128:256], dbt[:, 128:256], [r_dbt], [r_dbias])
        memset(bcat[:, h, 256:640], 0.0, [r_dbias])
        ts(bcat[:, h, 256:640], bcat[:, h, 256:640], b31[:, h:h + 1], None, ALU.add, None, [r_dbias, r_const], [r_dbias])
    memset(ones_bf[:], 1.0, [r_ones])
    memset(epsb[:, 0:1], 1e-6, [r_const])
    memset(epsb[:, 1:2], 1e-5, [r_const])
    pr, r_pr = sf()
    for l in range(depth):
        L = layer0 + l
        lam_init = 0.8 - 0.6 * math.exp(-0.3 * L)
        lt, r_lt = sf()
        dma("sp", lt[:, 0:256], lamrep_d[:, l * 256:(l + 1) * 256], [], [r_lt])
        base = 0
        tt(pr[:, 0:64], lt[:, base:base + 64], lt[:, base + 64:base + 128], ALU.mult, [r_lt], [r_pr])
        tt(pr[:, 64:128], lt[:, base + 128:base + 192], lt[:, base + 192:base + 256], ALU.mult, [r_lt], [r_pr])
        P.op("dve", lambda e, l=l: e.reduce_sum(out=lamt[:, l, 2:3], in_=pr[:, 0:64], axis=mybir.AxisListType.X), [r_pr], [r_lam])
        P.op("dve", lambda e, l=l: e.reduce_sum(out=lamt[:, l, 3:4], in_=pr[:, 64:128], axis=mybir.AxisListType.X), [r_pr], [r_lam])
        act(lamt[:, l, 2:4], lamt[:, l, 2:4], AF.Exp, [r_lam], [r_lam])
        stt(lamt[:, l, 0:1], lamt[:, l, 3:4], -lam_init, lamt[:, l, 2:3], ALU.add, ALU.subtract, [r_lam], [r_lam])
        ts(gsub[:, l:l + 1], colv[:, l * NCOLV + 130:l * NCOLV + 131], 1.0 - lam_init, None, ALU.mult, None,
           [r_colv], [r_lam])

    r_cast = [P.res(f"cast{l}") for l in range(depth)]

    def cast_layer(l):
        def c2(dst, src, rows, step):
            for r0 in range(0, rows, step):
                r1 = min(rows, r0 + step)
                dma("pool", dst[r0:r1, :], src[r0:r1, :], [], [r_cast[l]])
        c2(win_bf[l], w_in_d[l], D, 256)
        c2(wgate_bf[l], w_gate_d[l], D, 256)
        c2(proj_bf[l, 0:256], proj_a_d[l], 256, 256)
        c2(proj_bf[l, 256:512], proj_b_d[l], 256, 256)
        c2(proj_bf[l, 512:1024], proj_c_d[l], 512, 256)
        c2(wout_bf[l], w_out_d[l], D, 256)
        c2(wup_bf[l], w_up_d[l], D, 128)
        c2(wdn_bf[l], w_down_d[l], DFF, 256)

    def load_blk(l, src, nk, mw):
        i = ws_i[0] % NSLOT
        ws_i[0] += 1
        dma("sp", wslot[i][:, 0:nk, 0:mw], src, [r_cast[l]], [r_wslot[i]])
        return wslot[i], r_wslot[i]

    def wview(t2d):
        return t2d.rearrange("(kc p) m -> p kc m", p=128)

    def rmsnorm_to_hT(l, gcol):
        nb = 7
        for kc in range(8):
            sq, r_sq = sbf()
            act(sq[:], xT[:, kc, :], AF.Square, [r_xT[kc]], [r_sq])
            mm(bank[nb][:], ones_bf[:], sq[:], kc == 0, kc == 7, [r_ones, r_sq], [r_bank[nb]])
        sd, r_sd = sf()
        act(sd[:, 0:T], bank[nb][:], AF.Ln, [r_bank[nb], r_const], [r_sd], bias=epsb[:, 0:1], scale=1.0 / D)
        act(sd[:, 0:T], sd[:, 0:T], AF.Exp, [r_sd], [r_sd], scale=-0.5)
        for kc in range(8):
            stt(hT[:, kc, :], xT[:, kc, :], colv[:, gcol + kc:gcol + kc + 1], sd[:, 0:T], ALU.mult, ALU.mult,
                [r_xT[kc], r_colv, r_sd], [r_hT[kc]])

    scale = 64 ** -0.5

    cast_layer(0)
    for l in range(depth):
        P.epoch = l
        cb = l * NCOLV
        if l + 1 < depth:
            cast_layer(l + 1)
        dma("sp", lnrep[:], lnrep_d[l], [], [r_ln])
        dma("sp", bsT[:], bsT_d[l], [], [r_bsT])
        wst, r_wst = sf()
        dma("sp", wst[:, 0:512], wsT_d[l].rearrange("p g t -> p (g t)"), [], [r_wst])
        trl, r_trl = sf()
        dma("sp", trl[:, 0:128], tril_d, [], [r_trl])
        for g in range(4):
            tt(wsT[:, g, :], wst[:, g * 128:(g + 1) * 128], trl[:, 0:128], ALU.mult, [r_wst, r_trl], [r_wsT])
        pbd, r_pbd = sf()
        dma("sp", pbd[:, 0:256], poolbd_d[l].rearrange("p c q -> p (c q)"), [], [r_pbd])
        cp(poolbd[:].rearrange("p c q -> p (c q)"), pbd[:, 0:256], [r_pbd], [r_poolbd])
        for j in range(NFF):
            memset(carry[:, j, :], 0.0, [r_carry[j]])
        for s4 in range(4):
            memset(vA[:, s4, :], 0.0, [r_vAB[s4]])
            memset(vB[:, s4, :], 0.0, [r_vAB[s4]])

        for it in range(NT):
            t0 = it * T
            for kc in range(8):
                pass
            for kc in range(8):
                if l == 0:
                    src = xT_d[kc * 128:(kc + 1) * 128, t0:t0 + T]
                    P.op("sp", lambda e, src=src, kc=kc: e.dma_start(out=xT[:, kc, :], in_=src), [], [r_xT[kc]], dma=True)
                else:
                    P.op("sp", lambda e, it=it, kc=kc: e.dma_start(out=xT[:, kc, :], in_=xs_d[it, :, kc, :]),
                         [r_xs[kc]], [r_xT[kc]], dma=True)
            rmsnorm_to_hT(l, cb + 0)

            winv = wview(win_bf[l])
            wA, r_wA = load_blk(l, winv[:, :, 0:512], 8, 512)
            uT = [pb[:, 18:20, :].rearrange("p a t -> p (a t)").bitcast(F32),
                  pb[:, 20:22, :].rearrange("p a t -> p (a t)").bitcast(F32)]
            r_uT = [[r_pb[18], r_pb[19]], [r_pb[20], r_pb[21]]]
            for c in range(2):
                for kc in range(8):
                    mm(bank[c][:], wA[:, kc, c * 128:(c + 1) * 128], hT[:, kc, :], kc == 0, kc == 7,
                       [r_wA, r_hT[kc]], [r_bank[c]])
                act(uT[c], bank[c][:], AF.Gelu, [r_bank[c]], r_uT[c])
            vgs = []
            for s4 in range(4):
                bk = 2 + (s4 % 2)
                for kc in range(8):
                    mm(bank[bk][:, 0:256], hT[:, kc, s4 * 128:(s4 + 1) * 128], wA[:, kc, 256:512], kc == 0, kc == 7,
                       [r_wA, r_hT[kc]], [r_bank[bk]])
                vg, r_vg = sf()
                vgs.append((vg, r_vg))
                act(vg[:, 0:256], bank[bk][:, 0:256], AF.Gelu, [r_bank[bk]], [r_vg])
                P.op("dve", lambda e, vg=vg: e.bn_stats(out=small[:, 0:6], in_=vg[:, 0:256]), [r_vg], [r_small])
                P.op("dve", lambda e, s4=s4: e.bn_aggr(out=small[:, 8 + 2 * s4:10 + 2 * s4], in_=small[:, 0:6]),
                     [r_small], [r_small])
            var4 = small[:, 8:16].rearrange("p (s two) -> p s two", two=2)[:, :, 1]
            act(small2[:, 0:4], var4, AF.Ln, [r_small, r_const], [r_small], bias=epsb[:, 1:2])
            act(small2[:, 0:4], small2[:, 0:4], AF.Exp, [r_small], [r_small], scale=-0.5)

            def gview(ap2d, b):
                return ap2d.rearrange("p (a b d) -> p a b d", a=2, b=2)[:, :, b, :]
            for s4 in range(4):
                vg, r_vg = vgs[s4]
                ts(vg[:, 0:256], vg[:, 0:256], small[:, 8 + 2 * s4:9 + 2 * s4], small2[:, s4:s4 + 1], ALU.subtract, ALU.mult,
                   [r_vg, r_small], [r_vg])
                tt(vg[:, 0:256], vg[:, 0:256], lnrep[:, 0, :], ALU.mult, [r_vg, r_ln], [r_vg])
                tt(gview(vA[:, s4, :], 0), gview(vg[:, 0:256], 0), gview(lnrep[:, 1, :], 0), ALU.add,
                   [r_vg, r_ln], [r_vAB[s4]])
                tt(gview(vB[:, s4, :], 1), gview(vg[:, 0:256], 1), gview(lnrep[:, 1, :], 1), ALU.add,
                   [r_vg, r_ln], [r_vAB[s4]])
            wB, r_wB = load_blk(l, winv[:, :, 512:768], 8, 256)
            pT = [pb[:, 16, :], pb[:, 17, :]]
            ybT = [pb[:, 10, :], pb[:, 11, :]]
            for c in range(2):
                bk = 6 + c
                for kc in range(8):
                    mm(bank[bk][:], wB[:, kc, c * 128:(c + 1) * 128], hT[:, kc, :], kc == 0, kc == 7,
                       [r_wB, r_hT[kc]], [r_bank[bk]])
                if it == 0:
                    memset(XB[:, c, 0:16], 0.0, [r_XB[c]])
                else:
                    cp(XB[:, c, 0:16], XB[:, c, T:T + 16], [r_XB[c]], [r_XB[c]])
                P.op("act", lambda e, c=c, bk=bk: e.activation(out=XB[:, c, 16:16 + T], in_=bank[bk][:], func=AF.Copy),
                     [r_bank[bk]], [r_XB[c]])
                A_, r_A = sf()
                B_, r_B = sf()
                W_ = T + 16
                x_ = XB[:, c, :]
                tt(A_[:, 1:W_], x_[:, 1:W_], x_[:, 0:W_ - 1], ALU.add, [r_XB[c]], [r_A])
                if c == 0:
                    tt(B_[64:128, 3:W_], A_[64:128, 3:W_], A_[64:128, 1:W_ - 2], ALU.add, [r_A], [r_B])
                    lo, hi = A_, B_
                else:
                    tt(B_[:, 3:W_], A_[:, 3:W_], A_[:, 1:W_ - 2], ALU.add, [r_A], [r_B])
                    tt(A_[:, 7:W_], B_[:, 7:W_], B_[:, 3:W_ - 4], ALU.add, [r_B, r_A], [r_A])
                    tt(B_[64:128, 15:W_], A_[64:128, 15:W_], A_[64:128, 7:W_ - 8], ALU.add, [r_A, r_B], [r_B])
                    lo, hi = A_, B_
                for half, wt_ in ((0, lo), (1, hi)):
                    ps = slice(half * 64, half * 64 + 64)
                    stt(pT[c][ps, :], wt_[ps, 16:W_], constk[ps, c:c + 1], x_[ps, 16:W_], ALU.mult, ALU.subtract,
                        [r_A, r_B, r_XB[c], r_const], [r_pb[16 + c]])
                    if it == 0:
                        f_, r_f = sf()
                        tt(f_[ps, 0:16], wt_[ps, 16:32], constk[ps, 2 + c * 16:2 + c * 16 + 16], ALU.mult,
                           [r_A, r_B, r_const], [r_f])
                        tt(pT[c][ps, 0:16], f_[ps, 0:16], x_[ps, 16:32], ALU.subtract, [r_f, r_XB[c]], [r_pb[16 + c]])
            wQ, r_wQ = load_blk(l, winv[:, :, 768:1280], 8, 512)
            for h in range(4):
                bk = h % 4
                for kc in range(8):
                    mm(bank[bk][:], wQ[:, kc, h * 128:(h + 1) * 128], hT[:, kc, :], kc == 0, kc == 7,
                       [r_wQ, r_hT[kc]], [r_bank[bk]])
                P.op("act", lambda e, h=h, bk=bk: e.activation(out=QT[:, h, :], in_=bank[bk][:], func=AF.Copy),
                     [r_bank[bk]], [r_QT[h]])
            wK, r_wK = load_blk(l, winv[:, :, 1280:1792], 8, 512)
            for h in range(4):
                bk = 4 + h % 4
                for kc in range(8):
                    mm(bank[bk][:], wK[:, kc, h * 128:(h + 1) * 128], hT[:, kc, :], kc == 0, kc == 7,
                       [r_wK, r_hT[kc]], [r_bank[bk]])
                P.op("act", lambda e, h=h, bk=bk, t0=t0: e.activation(out=KT[:, h, t0:t0 + T], in_=bank[bk][:], func=AF.Copy),
                     [r_bank[bk]], [r_KT[h][it]])
            wV, r_wV = load_blk(l, winv[:, :, 1792:2304], 8, 512)
            for s4 in range(4):
                bk = s4 % 4
                kb = it * 4 + s4
                for kc in range(8):
                    mm(bank[bk][:], hT[:, kc, s4 * 128:(s4 + 1) * 128], wV[:, kc, :], kc == 0, kc == 7,
                       [r_wV, r_hT[kc]], [r_bank[bk]])
                cp(V[:, kb, :], bank[bk][:], [r_bank[bk]], [r_V[kb]])

            yaT = [pb[:, 8, :], pb[:, 9, :]]
            for c in range(2):
                bk = 4 + c
                for s4 in range(4):
                    mm(bank[bk][:, s4 * 128:(s4 + 1) * 128], vA[:, s4, c * 128:(c + 1) * 128], wsT[:, 2 * c, :],
                       True, False, [r_vAB[s4], r_wsT], [r_bank[bk]])
                    mm(bank[bk][:, s4 * 128:(s4 + 1) * 128], vB[:, s4, c * 128:(c + 1) * 128], wsT[:, 2 * c + 1, :],
                       False, True, [r_vAB[s4], r_wsT], [r_bank[bk]])
                tm, r_tm = sf()
                tt(tm[:, 0:T].rearrange("p (s t) -> p s t", s=4), bank[bk][:].rearrange("p (s t) -> p s t", s=4),
                   bsT[:, c, :].unsqueeze(1).broadcast_to([128, 4, 128]), ALU.add, [r_bank[bk], r_bsT], [r_tm])
                tt(yaT[c], tm[:, 0:T], uT[c], ALU.mult, [r_tm] + r_uT[c], [r_pb[8 + c]])
            if l == 0 and it == 0:
                dump("yaT", pb[:, 8:10, :], [r_pb[8], r_pb[9]], [128, 2, T], BF16)

            for c in range(2):
                bk2 = 2 + c
                mm(bank[bk2][:], poolbd[:, c, :], pT[c], True, True, [r_poolbd, r_pb[16 + c]], [r_bank[bk2]])
                ts(ybT[c], bank[bk2][:], colv[:, cb + 128 + c:cb + 129 + c], None, ALU.mult, None,
                   [r_bank[bk2], r_colv], [r_pb[10 + c]])
            if l == 0 and it == 0:
                dump("ybT", pb[:, 10:12, :], [r_pb[10], r_pb[11]], [128, 2, T], BF16)

            ycT = [pb[:, 12 + h, :] for h in range(4)]
            nkb = 4 * it + 4
            deferred = [None]
            for h in range(4):
                SB = [0, 1, 7]
                OB = [2, 3]
                RB = [4, 5]
                NB_ = 6
                pend = []
                sidx = 0
                pidx = 0

                def flush(pend):
                    for (c, kb, q0, PTc, r_PT) in pend:
                        mm(bank[OB[c]][:, q0:T], V[:, kb, h * 128:(h + 1) * 128], PTc[:, q0:T], kb == 0, kb == nkb - 1,
                           [r_V[kb], r_PT], [r_bank[OB[c]]])
                        mm(bank[RB[c]][:, q0:T], ones_bf[:], PTc[:, q0:T], kb == 0, kb == nkb - 1,
                           [r_ones, r_PT], [r_bank[RB[c]]])

                for kb in range(nkb):
                    j = kb - 4 * it
                    q0 = 128 * max(0, j)
                    kit = kb // 4
                    PT = ptb[pidx % 3]
                    r_PT = r_ptb[pidx % 3]
                    pidx += 1
                    sbks = []
                    for c in range(2):
                        sbk = SB[sidx % 3]
                        sidx += 1
                        sbks.append(sbk)
                        cs = slice(c * 64, c * 64 + 64)
                        mm(bank[sbk][:, q0:T], KT[cs, h, kb * 128:(kb + 1) * 128], QT[cs, h, q0:T], True, True,
                           [r_KT[h][kit], r_QT[h]], [r_bank[sbk]])
                    for c in range(2):
                        sbk = sbks[c]
                        if j < -1:
                            act(PT[:, c, q0:T], bank[sbk][:, q0:T], AF.Exp, [r_bank[sbk], r_const], [r_PT],
                                bias=b31[:, h:h + 1], scale=scale)
                        else:
                            tmp, r_tmp = sf()
                            off = 128 if j == -1 else 0
                            stt(tmp[:, q0:T], bank[sbk][:, q0:T], scale, bcat[:, h, off:off + T - q0], ALU.mult, ALU.add,
                                [r_bank[sbk], r_dbias], [r_tmp])
                            act(PT[:, c, q0:T], tmp[:, q0:T], AF.Exp, [r_tmp], [r_PT])
                    newp = [(c, kb, q0, PT[:, c, :], r_PT) for c in range(2)]
                    flush(pend)
                    pend = newp
                    if kb == 1 and deferred[0] is not None:
                        deferred[0]()
                        deferred[0] = None
                flush(pend)
                i0, r_i0 = sf_t[NSF - 2], r_sf[NSF - 2]
                i1, r_i1 = sf_t[NSF - 1], r_sf[NSF - 1]
                act(i0[:, 0:T], bank[RB[0]][:], AF.Ln, [r_bank[RB[0]]], [r_i0])
                act(i1[:, 0:T], bank[RB[1]][:], AF.Ln, [r_bank[RB[1]]], [r_i1])
                act(i0[:, 0:T], i0[:, 0:T], AF.Exp, [r_i0], [r_i0], scale=-1.0)
                act(i1[:, 0:T], i1[:, 0:T], AF.Exp, [r_i1], [r_i1], scale=-1.0)
                tt(i0[:, 0:T], bank[OB[0]][:], i0[:, 0:T], ALU.mult, [r_bank[OB[0]], r_i0], [r_i0])
                tt(i1[:, 0:T], bank[OB[1]][:], i1[:, 0:T], ALU.mult, [r_bank[OB[1]], r_i1], [r_i1])
                stt(i0[:, 0:T], i1[:, 0:T], lamt[:, l, 0:1], i0[:, 0:T], ALU.mult, ALU.add, [r_i0, r_i1, r_lam], [r_i0])
                sq, r_sq = sbf()
                tt(sq[:], i0[:, 0:T], i0[:, 0:T], ALU.mult, [r_i0], [r_sq], eng="pool" if POOL_SQ else "dve")

                def tail(h=h, sq=sq, r_sq=r_sq, i0=i0, r_i0=r_i0, i1=i1, r_i1=r_i1, NB_=NB_):
                    mm(bank[NB_][:], ones_bf[:], sq[:], True, True, [r_ones, r_sq], [r_bank[NB_]])
                    act(i1[:, 0:T], bank[NB_][:], AF.Ln, [r_bank[NB_], r_const], [r_i1], bias=epsb[:, 1:2], scale=1.0 / 128)
                    act(i1[:, 0:T], i1[:, 0:T], AF.Exp, [r_i1], [r_i1], scale=-0.5)
                    stt(ycT[h], i0[:, 0:T], gsub[:, l:l + 1], i1[:, 0:T], ALU.mult, ALU.mult, [r_i0, r_i1, r_lam],
                        [r_pb[12 + h]])
                if h < 3:
                    deferred[0] = tail
                else:
                    tail()
            if l == 0 and it == 0:
                dump("ycT", pb[:, 12:16, :], [r_pb[12 + h] for h in range(4)], [128, 4, T], BF16)

            wgv = wview(wgate_bf[l])
            prv = wview(proj_bf[l])
            for half in range(2):
                wg = []
                for br in range(3):
                    c0 = br * 1024 + half * 512
                    wg.append(load_blk(l, wgv[:, :, c0:c0 + 512], 8, 512))
                wp, r_wp = load_blk(l, prv[:, :, half * 512:half * 512 + 512], 8, 512)
                for ml in range(4):
                    m = half * 4 + ml
                    cs = slice(ml * 128, ml * 128 + 128)
                    sg = []
                    for br in range(3):
                        for kc in range(8):
                            mm(bank[br][:], wg[br][0][:, kc, cs], hT[:, kc, :], kc == 0, kc == 7,
                               [wg[br][1], r_hT[kc]], [r_bank[br]])
                        s_, r_s = sf()
                        act(s_[:, 0:T], bank[br][:], AF.Sigmoid, [r_bank[br], r_colv], [r_s],
                            bias=colv[:, cb + 16 + br * 8 + m:cb + 17 + br * 8 + m])
                        sg.append((s_, r_s))
                    srcs = [([pb[:, 8, :], pb[:, 9, :]], [r_pb[8], r_pb[9]], 0),
                            ([pb[:, 10, :], pb[:, 11, :]], [r_pb[10], r_pb[11]], 2),
                            ([pb[:, 12 + h, :] for h in range(4)], [r_pb[12 + h] for h in range(4)], 4)]
                    for br in range(3):
                        ys, rys, k0 = srcs[br]
                        for i_, (y_, ry_) in enumerate(zip(ys, rys)):
                            mm(bank[3 + br][:], wp[:, k0 + i_, cs], y_, i_ == 0, i_ == len(ys) - 1,
                               [r_wp, ry_], [r_bank[3 + br]])
                        s_, r_s = sg[br]
                        tt(s_[:, 0:T], bank[3 + br][:], s_[:, 0:T], ALU.mult, [r_bank[3 + br], r_s], [r_s])
                    tt(sg[0][0][:, 0:T], sg[0][0][:, 0:T], sg[1][0][:, 0:T], ALU.add, [sg[0][1], sg[1][1]], [sg[0][1]])
                    tt(pb[:, m, :], sg[0][0][:, 0:T], sg[2][0][:, 0:T], ALU.add, [sg[0][1], sg[2][1]], [r_pb[m]])
            if l == 0 and it == 0:
                dump("mergedT", pb[:, 0:8, :], [r_pb[m] for m in range(8)], [128, 8, T], BF16)

            wov = wview(wout_bf[l])
            for half in range(2):
                wo, r_wo = load_blk(l, wov[:, :, half * 512:half * 512 + 512], 8, 512)
                for ml in range(4):
                    m = half * 4 + ml
                    bk = 6 + (ml % 2)
                    for kc in range(8):
                        mm(bank[bk][:], wo[:, kc, ml * 128:(ml + 1) * 128], pb[:, kc, :], kc == 0, kc == 7,
                           [r_wo, r_pb[kc]], [r_bank[bk]])
                    tt(xT[:, m, :], xT[:, m, :], bank[bk][:], ALU.add, [r_xT[m], r_bank[bk]], [r_xT[m]])
            if l == 0 and it == 0:
                dump("x1T", xT[:], r_xT, [128, 8, T], F32)

            rmsnorm_to_hT(l, cb + 8)
            wuv = wview(wup_bf[l])
            cw = cb + 40

            def ffn_stage_a(j, wG, r_wG, wU, r_wU, jj):
                gb = j % 3
                ub = 3 + j % 3
                cs = slice(jj * 128, jj * 128 + 128)
                for kc in range(8):
                    mm(bank[gb][:], wG[:, kc, cs], hT[:, kc, :], kc == 0, kc == 7, [r_wG, r_hT[kc]], [r_bank[gb]])
                for kc in range(8):
                    mm(bank[ub][:], wU[:, kc, cs], hT[:, kc, :], kc == 0, kc == 7, [r_wU, r_hT[kc]], [r_bank[ub]])
                G_, r_G = sf()
                cp(G_[:, 0:2], carry[:, j, :], [r_carry[j]], [r_G])
                P.op("act", lambda e, G_=G_, gb=gb: e.activation(out=G_[:, 2:2 + T], in_=bank[gb][:], func=AF.Copy),
                     [r_bank[gb]], [r_G])
                cp(carry[:, j, :], G_[:, T:T + 2], [r_G], [r_carry[j]])
                a_, r_a = sf()
                ts(a_[:, 0:T], G_[:, 0:T], colv[:, cw + j:cw + j + 1], colv[:, cb + 106 + j:cb + 107 + j],
                   ALU.mult, ALU.add, [r_G, r_colv], [r_a])
                stt(a_[:, 0:T], G_[:, 1:T + 1], colv[:, cw + 22 + j:cw + 23 + j], a_[:, 0:T], ALU.mult, ALU.add,
                    [r_G, r_a, r_colv], [r_a])
                stt(a_[:, 0:T], G_[:, 2:T + 2], colv[:, cw + 44 + j:cw + 45 + j], a_[:, 0:T], ALU.mult, ALU.add,
                    [r_G, r_a, r_colv], [r_a])
                return (j, a_, r_a, ub)

            def ffn_stage_b(stt_):
                j, a_, r_a, ub = stt_
                act(a_[:, 0:T], a_[:, 0:T], AF.Gelu, [r_a], [r_a])
                tt(pb[:, j, :], a_[:, 0:T], bank[ub][:], ALU.mult, [r_a, r_bank[ub]], [r_pb[j]])

            prev = None
            for jb in range(6):
                mw = 512 if jb < 5 else 256
                wG, r_wG = load_blk(l, wuv[:, :, jb * 512:jb * 512 + mw], 8, mw)
                wU, r_wU = load_blk(l, wuv[:, :, DFF + jb * 512:DFF + jb * 512 + mw], 8, mw)
                for jj in range(mw // 128):
                    j = jb * 4 + jj
                    cur = ffn_stage_a(j, wG, r_wG, wU, r_wU, jj)
                    if prev is not None:
                        ffn_stage_b(prev)
                    prev = cur
            ffn_stage_b(prev)
            wdv = wview(wdn_bf[l])
            for mb in range(2):
                for kt in range(3):
                    nk = 8 if kt < 2 else 6
                    wd, r_wd = load_blk(l, wdv[:, kt * 8:kt * 8 + nk, mb * 512:mb * 512 + 512], nk, 512)
                    for ml in range(4):
                        bk = 4 + ml
                        for kk in range(nk):
                            kc = kt * 8 + kk
                            mm(bank[bk][:], wd[:, kk, ml * 128:(ml + 1) * 128], pb[:, kc, :], kc == 0, kc == NFF - 1,
                               [r_wd, r_pb[kc]], [r_bank[bk]])
                for ml in range(4):
                    m = mb * 4 + ml
                    tt(xT[:, m, :], xT[:, m, :], bank[4 + ml][:], ALU.add, [r_xT[m], r_bank[4 + ml]], [r_xT[m]])
                    if l < depth - 1:
                        P.op("act", lambda e, it=it, m=m: e.dma_start(out=xs_d[it, :, m, :], in_=xT[:, m, :]),
                             [r_xT[m]], [r_xs[m]], dma=True)

            if l == depth - 1:
                gcol = depth * NCOLV
                nb = 7
                for kc in range(8):
                    sq, r_sq = sbf()
                    act(sq[:], xT[:, kc, :], AF.Square, [r_xT[kc]], [r_sq])
                    mm(bank[nb][:], ones_bf[:], sq[:], kc == 0, kc == 7, [r_ones, r_sq], [r_bank[nb]])
                sd, r_sd = sf()
                act(sd[:, 0:T], bank[nb][:], AF.Ln, [r_bank[nb], r_const], [r_sd], bias=epsb[:, 0:1], scale=1.0 / D)
                act(sd[:, 0:T], sd[:, 0:T], AF.Exp, [r_sd], [r_sd], scale=-0.5)
                for kc in range(8):
                    stt(xT[:, kc, :], xT[:, kc, :], colv[:, gcol + kc:gcol + kc + 1], sd[:, 0:T], ALU.mult, ALU.mult,
                        [r_xT[kc], r_colv, r_sd], [r_xT[kc]])
                dst = outT_d.rearrange("(kc p) t -> p kc t", p=128)[:, :, t0:t0 + T]
                P.op("sp", lambda e, dst=dst: e.dma_start(out=dst, in_=xT[:]), r_xT, [r_out], dma=True)

    P.epoch = depth
    fin = [r_out] + list(dbg_out.values())
    P.op("sp", lambda e: None, fin, [])
    P.emit(st)
    st.close()
    return nc, P


def _t5_bucket_np(rel):
    n = np.maximum(rel, 0)
    nf = np.maximum(n, 1).astype(np.float32)
    large = 16 + (np.log(nf / np.float32(16)) / np.float32(math.log(128 / 16)) * np.float32(16)).astype(np.int32)
    large = np.minimum(large, 31)
    return np.where(n < 16, n, large)


def _bucket_tables():
    k = np.arange(128)[:, None]
    q = np.arange(128)[None, :]
    try:
        import jax
        import jax.numpy as jnp

        def bk(rel):
            nn = jnp.maximum(rel, 0)
            nf = jnp.maximum(nn, 1).astype(jnp.float32)
            large = 16 + (jnp.log(nf / 16) / math.log(128 / 16) * 16).astype(jnp.int32)
            large = jnp.minimum(large, 31)
            return np.asarray(jnp.where(nn < 16, nn, large))
        with jax.default_device(jax.devices("cpu")[0]):
            b0 = bk(jnp.asarray(q - k))
            b1 = bk(jnp.asarray(128 + q - k))
        return b0, b1
    except Exception:
        return _t5_bucket_np(q - k), _t5_bucket_np(128 + q - k)


def prep_inputs(inp, depth, b, S):
    f = np.float32
    m = {}
    m["xT"] = np.ascontiguousarray(inp["x"][b, :S].T)
    for k in ("w_in", "w_gate", "proj_a", "proj_b", "proj_c", "w_out", "w_up", "w_down"):
        m[k] = np.ascontiguousarray(inp[k][:depth])
    NCOL = depth * NCOLV + 8
    colv = np.zeros((128, NCOL), f)
    for l in range(depth):
        cb = l * NCOLV
        colv[:, cb + 0:cb + 8] = inp["attn_norm_g"][l].reshape(8, 128).T
        colv[:, cb + 8:cb + 16] = inp["ffn_norm_g"][l].reshape(8, 128).T
        colv[:, cb + 16:cb + 40] = inp["b_gate"][l].reshape(24, 128).T
        for tap in range(3):
            colv[:, cb + 40 + tap * 22:cb + 62 + tap * 22] = inp["conv_w"][l, tap].reshape(22, 128).T
        colv[:, cb + 106:cb + 128] = inp["conv_b"][l].reshape(22, 128).T
        colv[:, cb + 128:cb + 130] = inp["pool_scale"][l].reshape(2, 128).T
        colv[:, cb + 130] = inp["diff_subln_g"][l]
    colv[:, depth * NCOLV:depth * NCOLV + 8] = inp["final_norm_g"].reshape(8, 128).T
    m["colv"] = colv
    m["lamrep"] = np.ascontiguousarray(np.broadcast_to(inp["diff_lam"][:depth].reshape(1, depth * 256), (128, depth * 256)))
    m["b31rep"] = np.ascontiguousarray(np.broadcast_to(inp["rel_bias"][31][None, :], (128, 4)))
    b0, b1 = _bucket_tables()
    rb = inp["rel_bias"]
    dbias = np.zeros((128, 4, 2, 128), f)
    for h in range(4):
        dbias[:, h, 0, :] = rb[b0, h]
        dbias[:, h, 1, :] = rb[b1, h]
    m["dbias"] = dbias
    k = np.arange(128)[:, None]
    q = np.arange(128)[None, :]
    m["maskneg"] = np.where(q >= k, 0.0, -30000.0).astype(f)
    m["tril01"] = np.where(q >= k, 1.0, 0.0).astype(f)
    ln = np.zeros((depth, 128, 2, 256), f)
    ln[:, :, 0, :] = inp["sgu_ln_g"][:depth, None, :]
    ln[:, :, 1, :] = inp["sgu_ln_b"][:depth, None, :]
    m["lnrep"] = ln
    bs = np.zeros((depth, 128, 2, 128), f)
    for c in range(2):
        for hf in range(2):
            bs[:, hf * 64:(hf + 1) * 64, c, :] = inp["sgu_b"][:depth, 2 * c + hf][:, None, :]
    m["bsT"] = bs
    m["wsT"] = np.ascontiguousarray(np.transpose(inp["sgu_w"][:depth], (0, 3, 1, 2)))
    pbd = np.zeros((depth, 128, 2, 128), f)
    for c in range(2):
        for hf in range(2):
            pbd[:, hf * 64:(hf + 1) * 64, c, hf * 64:(hf + 1) * 64] = inp["pool_w"][:depth, 2 * c + hf]
    m["poolbd"] = pbd
    ck = np.zeros((128, 34), f)
    wins = [[2, 4], [8, 16]]
    for c in range(2):
        for hf in range(2):
            w = wins[c][hf]
            ck[hf * 64:(hf + 1) * 64, c] = 1.0 / w
            ck[hf * 64:(hf + 1) * 64, 2 + c * 16:2 + c * 16 + 16] = 1.0 / np.minimum(np.arange(16) + 1, w)
    m["constk"] = ck
    return m


_CACHE = {}


def kernel(**inputs):
    inp = {k: np.asarray(v) for k, v in inputs.items()}
    B, S, _ = inp["x"].shape
    depth = inp["w_in"].shape[0]
    key = (S, depth)
    if key not in _CACHE:
        _CACHE[key] = build(S=S, depth=depth)[0]
    nc = _CACHE[key]
    in_maps = [prep_inputs(inp, depth, b, S) for b in range(B)]
    res = run_bass_kernel_spmd(nc, in_maps, core_ids=list(range(B)))
    out = np.stack([np.asarray(r["outT"]).T for r in res.results], axis=0)
    return np.ascontiguousarray(out.astype(np.float32))
```
